# Optimizing a Trainium2 kernel written in Bass

```python
import jax, jax.numpy as jnp
from jax import lax
import numpy as np

D_MODEL = 1024
BATCH = 8
SEQ = 8192
DEPTH = 2

D_PLE = 256
CHUNK = 64

SSD_HEADS = 8
SSD_HEAD_DIM = 64
SSD_WIDTH = SSD_HEADS * SSD_HEAD_DIM
SSD_GROUPS = 2
SSD_STATE = 128
SSD_CONV = 5
SSD_CONV_CH = SSD_WIDTH + 2 * SSD_GROUPS * SSD_STATE

GLA_HEADS = 4
GLA_DK = 32
GLA_DV = 64
GLA_WIDTH = GLA_HEADS * GLA_DV
GLA_GATE_RANK = 16
GLA_GATE_TEMP = 16.0

RET_HEADS = 4
RET_DK = 64
RET_DV = 64
RET_WIDTH = RET_HEADS * RET_DV
ROPE_BASE = 10000.0

D_MIX = SSD_WIDTH + GLA_WIDTH + RET_WIDTH

SPLIT_SIZES = [
    SSD_WIDTH,
    SSD_CONV_CH,
    2 * SSD_HEADS,
    GLA_HEADS * GLA_DK,
    GLA_HEADS * GLA_DK,
    GLA_WIDTH,
    GLA_WIDTH,
    2 * GLA_GATE_RANK,
    RET_HEADS * RET_DK,
    RET_HEADS * RET_DK,
    RET_WIDTH,
    RET_WIDTH,
]
N_IN = int(sum(SPLIT_SIZES))
SPLIT_IDX = [int(i) for i in np.cumsum(SPLIT_SIZES)[:-1]]

DN_ALPHA = float((2 * DEPTH) ** 0.25)
DN_BETA = float((8 * DEPTH) ** -0.25)
LN_EPS = 1e-5
RMS_EPS = 1e-6

kernel_name = "bidir_hybrid_ssd_gla_retention_deepnorm"


def layer_norm(x, w, b):
    xf = x.astype(jnp.float32)
    mu = jnp.mean(xf, axis=-1, keepdims=True)
    var = jnp.mean(jnp.square(xf - mu), axis=-1, keepdims=True)
    return ((xf - mu) * lax.rsqrt(var + LN_EPS) * w + b).astype(x.dtype)


def rms_norm(x, w):
    xf = x.astype(jnp.float32)
    return (xf * lax.rsqrt(jnp.mean(jnp.square(xf), axis=-1, keepdims=True) + RMS_EPS) * w).astype(x.dtype)


def centred_depthwise_conv(x, w, b):
    k = w.shape[0]
    y = lax.conv_general_dilated(
        x, w[:, None, :].astype(x.dtype), window_strides=(1,),
        padding=[(k // 2, k // 2)], dimension_numbers=("NWC", "WIO", "NWC"),
        feature_group_count=x.shape[-1])
    return y + b


def _to_chunks(t):
    return t.astype(jnp.float32).reshape(t.shape[0], t.shape[1] // CHUNK, CHUNK, *t.shape[2:])


def _carry_states(chunk_states, chunk_decay):
    def step(carry, inp):
        s, d = inp
        return carry * d[..., None] + s, carry
    init = jnp.zeros_like(chunk_states[:, 0])
    _, entering = lax.scan(step, init, (jnp.moveaxis(chunk_states, 1, 0), jnp.moveaxis(chunk_decay, 1, 0)))
    return jnp.moveaxis(entering, 0, 1)


def _mask(inclusive):
    return jnp.tril(jnp.ones((CHUNK, CHUNK), dtype=bool), k=0 if inclusive else -1)


def scalar_decay_scan(q, k, v, log_a, inclusive):
    b, s_len, h, _ = q.shape
    q, k, v, la = _to_chunks(q), _to_chunks(k), _to_chunks(v), _to_chunks(log_a)
    cum = jnp.cumsum(la, axis=2)
    seg = cum[:, :, :, None, :] - cum[:, :, None, :, :]
    m = _mask(inclusive)[None, None, :, :, None]
    decay = jnp.exp(jnp.where(m, seg, -jnp.inf))
    scores = jnp.einsum("bnthk,bnshk->bntsh", q, k) * decay
    y = jnp.einsum("bntsh,bnshv->bnthv", scores, v)
    last = cum[:, :, -1]
    w_state = jnp.exp(last[:, :, None, :] - cum)
    states = jnp.einsum("bnsh,bnshk,bnshv->bnhkv", w_state, k, v)
    entering = _carry_states(states, jnp.exp(last)[..., None])
    y = y + jnp.einsum("bnthk,bnhkv->bnthv", q * jnp.exp(cum)[..., None], entering)
    return y.reshape(b, s_len, h, v.shape[-1])


def vector_decay_scan(q, k, v, log_a, inclusive):
    b, s_len, h, _ = q.shape
    q, k, v, la = _to_chunks(q), _to_chunks(k), _to_chunks(v), _to_chunks(log_a)
    cum = jnp.cumsum(la, axis=2)
    q_dec = q * jnp.exp(cum)
    scores = jnp.einsum("bnthk,bnshk->bntsh", q_dec, k * jnp.exp(-cum))
    scores = jnp.where(_mask(inclusive)[None, None, :, :, None], scores, 0.0)
    y = jnp.einsum("bntsh,bnshv->bnthv", scores, v)
    last = cum[:, :, -1]
    states = jnp.einsum("bnshk,bnshv->bnhkv", k * jnp.exp(last[:, :, None] - cum), v)
    entering = _carry_states(states, jnp.exp(last))
    y = y + jnp.einsum("bnthk,bnhkv->bnthv", q_dec, entering)
    return y.reshape(b, s_len, h, v.shape[-1])


def bidirectional(scan_fn, q, k, v_f, v_b, la_f, la_b):
    flip = lambda t: jnp.flip(t, axis=1)
    fwd = scan_fn(q, k, v_f, la_f, True)
    bwd = flip(scan_fn(flip(q), flip(k), flip(v_b), flip(la_b), False))
    return fwd + bwd


def rotary(t):
    s_len, d = t.shape[1], t.shape[-1]
    half = d // 2
    inv = ROPE_BASE ** (-jnp.arange(half, dtype=jnp.float32) / half)
    ang = jnp.arange(s_len, dtype=jnp.float32)[:, None] * inv[None, :]
    cos, sin = jnp.cos(ang)[None, :, None, :], jnp.sin(ang)[None, :, None, :]
    t1, t2 = t[..., :half].astype(jnp.float32), t[..., half:].astype(jnp.float32)
    return jnp.concatenate([t1 * cos - t2 * sin, t1 * sin + t2 * cos], axis=-1)


def ssd_branch(z, xbc, dt_raw, conv_w, conv_b, dt_bias, a_log, d_skip, norm_w):
    b, s_len, _ = z.shape
    xbc = jax.nn.silu(centred_depthwise_conv(xbc, conv_w, conv_b))
    xs, bm, cm = jnp.split(xbc, [SSD_WIDTH, SSD_WIDTH + SSD_GROUPS * SSD_STATE], axis=-1)
    xs = xs.reshape(b, s_len, SSD_HEADS, SSD_HEAD_DIM)
    rep = SSD_HEADS // SSD_GROUPS
    bh = jnp.repeat(bm.reshape(b, s_len, SSD_GROUPS, SSD_STATE), rep, axis=2)
    ch = jnp.repeat(cm.reshape(b, s_len, SSD_GROUPS, SSD_STATE), rep, axis=2)
    dt = jax.nn.softplus(dt_raw.astype(jnp.float32).reshape(b, s_len, 2, SSD_HEADS) + dt_bias)
    la = dt * (-jnp.exp(a_log))
    xf = xs.astype(jnp.float32)
    y = bidirectional(scalar_decay_scan, ch, bh,
                      xf * dt[:, :, 0, :, None], xf * dt[:, :, 1, :, None],
                      la[:, :, 0], la[:, :, 1])
    y = (y + d_skip[:, None] * xf).reshape(b, s_len, SSD_WIDTH)
    return rms_norm(y * jax.nn.silu(z.astype(jnp.float32)), norm_w)


def gla_branch(q, k, v, g, a_lr, w_a2, b_a, norm_w):
    b, s_len, _ = q.shape
    q = q.reshape(b, s_len, GLA_HEADS, GLA_DK) * (GLA_DK ** -0.5)
    k = k.reshape(b, s_len, GLA_HEADS, GLA_DK)
    v = v.reshape(b, s_len, GLA_HEADS, GLA_DV)
    a_lr = a_lr.astype(jnp.float32).reshape(b, s_len, 2, GLA_GATE_RANK)
    la = jax.nn.log_sigmoid(jnp.einsum("bsdr,drk->bsdk", a_lr, w_a2) + b_a) / GLA_GATE_TEMP
    la = la.reshape(b, s_len, 2, GLA_HEADS, GLA_DK)
    o = bidirectional(vector_decay_scan, q, k, v, v, la[:, :, 0], la[:, :, 1])
    o = rms_norm(o, norm_w).reshape(b, s_len, GLA_WIDTH)
    return o * jax.nn.silu(g.astype(jnp.float32))


def retention_branch(q, k, v, g, norm_w, norm_b):
    b, s_len, _ = q.shape
    q = rotary(q.reshape(b, s_len, RET_HEADS, RET_DK))
    k = rotary(k.reshape(b, s_len, RET_HEADS, RET_DK)) * (RET_DK ** -0.5)
    v = v.reshape(b, s_len, RET_HEADS, RET_DV)
    log_gamma = jnp.log(1.0 - 2.0 ** (-5.0 - jnp.arange(RET_HEADS, dtype=jnp.float32)))
    la = jnp.broadcast_to(log_gamma, (b, s_len, RET_HEADS))
    o = bidirectional(scalar_decay_scan, q, k, v, v, la, la)
    o = layer_norm(o, norm_w.reshape(RET_HEADS, RET_DV), norm_b.reshape(RET_HEADS, RET_DV))
    return o.reshape(b, s_len, RET_WIDTH) * jax.nn.silu(g.astype(jnp.float32))


def setup_inputs(seed: int = 0) -> dict:
    key = jax.random.key(seed)
    ks = jax.random.split(key, 24)
    f32 = jnp.float32
    nrm = lambda k, shape, scale: jax.random.normal(k, shape, f32) * scale
    dt0 = jnp.exp(jax.random.uniform(ks[5], (DEPTH, 2, SSD_HEADS), f32, np.log(1e-3), np.log(1e-1)))
    return {
        "x": nrm(ks[0], (BATCH, SEQ, D_MODEL), 1.0),
        "p": nrm(ks[1], (DEPTH, BATCH, SEQ, D_PLE), 1.0),
        "w_in": nrm(ks[2], (DEPTH, D_MODEL, N_IN), D_MODEL ** -0.5),
        "conv_w": nrm(ks[3], (DEPTH, SSD_CONV, SSD_CONV_CH), SSD_CONV ** -0.5),
        "conv_b": nrm(ks[4], (DEPTH, SSD_CONV_CH), 0.02),
        "dt_bias": dt0 + jnp.log(-jnp.expm1(-dt0)),
        "a_log": jnp.log(jax.random.uniform(ks[6], (DEPTH, 2, SSD_HEADS), f32, 1.0, 16.0)),
        "d_skip": 1.0 + nrm(ks[7], (DEPTH, SSD_HEADS), 0.1),
        "ssd_norm_w": 1.0 + nrm(ks[8], (DEPTH, SSD_WIDTH), 0.02),
        "gla_w_a2": nrm(ks[9], (DEPTH, 2, GLA_GATE_RANK, GLA_HEADS * GLA_DK), GLA_GATE_RANK ** -0.5),
        "gla_b_a": nrm(ks[10], (DEPTH, 2, GLA_HEADS * GLA_DK), 0.1),
        "gla_norm_w": 1.0 + nrm(ks[11], (DEPTH, GLA_DV), 0.02),
        "ret_norm_w": 1.0 + nrm(ks[12], (DEPTH, RET_WIDTH), 0.02),
        "ret_norm_b": nrm(ks[13], (DEPTH, RET_WIDTH), 0.02),
        "w_out": nrm(ks[14], (DEPTH, D_MIX, D_MODEL), DN_BETA * D_MIX ** -0.5),
        "ln_w": 1.0 + nrm(ks[15], (DEPTH, D_MODEL), 0.02),
        "ln_b": nrm(ks[16], (DEPTH, D_MODEL), 0.02),
        "w_pe": nrm(ks[17], (DEPTH, D_PLE, D_MODEL), D_PLE ** -0.5),
        "w_pg": nrm(ks[18], (DEPTH, D_MODEL, D_MODEL), D_MODEL ** -0.5),
        "b_pg": nrm(ks[19], (DEPTH, D_MODEL), 0.02),
    }


def reference(x, p, w_in, conv_w, conv_b, dt_bias, a_log, d_skip, ssd_norm_w,
              gla_w_a2, gla_b_a, gla_norm_w, ret_norm_w, ret_norm_b, w_out,
              ln_w, ln_b, w_pe, w_pg, b_pg):
    h = x
    for i in range(DEPTH):
        u = jnp.einsum("bsd,dn->bsn", h, w_in[i])
        (z, xbc, dt_raw, gq, gk, gv, gg, ga, rq, rk, rv, rg) = jnp.split(u, SPLIT_IDX, axis=-1)
        y_ssd = ssd_branch(z, xbc, dt_raw, conv_w[i], conv_b[i], dt_bias[i], a_log[i],
                           d_skip[i], ssd_norm_w[i])
        y_gla = gla_branch(gq, gk, gv, gg, ga, gla_w_a2[i], gla_b_a[i], gla_norm_w[i])
        y_ret = retention_branch(rq, rk, rv, rg, ret_norm_w[i], ret_norm_b[i])
        y_cat = jnp.concatenate([y_ssd, y_gla, y_ret], axis=-1).astype(h.dtype)
        mix = jnp.einsum("bsm,md->bsd", y_cat, w_out[i])
        h = layer_norm(DN_ALPHA * h + mix, ln_w[i], ln_b[i])
        gate = jax.nn.sigmoid(jnp.einsum("bsd,de->bse", h, w_pg[i]) + b_pg[i])
        h = h + gate * jnp.einsum("bsp,pd->bsd", p[i], w_pe[i])
    return h
```

```python
import contextlib
import os
import numpy as np
import concourse.bass as bass
import concourse.mybir as mybir
from concourse.bass_utils import run_bass_kernel_spmd

F32 = mybir.dt.float32
BF16 = mybir.dt.bfloat16
AF = mybir.ActivationFunctionType
ALU = mybir.AluOpType
AX = mybir.AxisListType

D_MODEL = 1024
SEQ = 8192
DEPTH = 2
D_PLE = 256
L = 128
SC = 512
DN_ALPHA = float((2 * DEPTH) ** 0.25)
LN_EPS = 1e-5
RMS_EPS = 1e-6
NEG = -30000.0

N_FM = 11
TM_GROUPS = [("z", 512), ("gkvd", 400), ("gate", 512), ("rqk", 512), ("rv", 256)]
TM_OFF = {}
_o = N_FM * 128
for _n, _w in TM_GROUPS:
    TM_OFF[_n] = _o
    _o += _w
NA = _o

VEC = {}
_o = 0
for _n, _w in [("dtb", 16), ("alog", 16), ("dsk", 8), ("ssd_nw", 512), ("ba", 256), ("gla_nw", 64),
               ("ret_nw", 256), ("ret_nb", 256), ("ln_w", 1024), ("ln_b", 1024), ("b_pg", 1024)]:
    VEC[_n] = (_o, _w)
    _o += _w
NV = _o

CST = {}
_o = 0
for _n, _w in [("ident", 128), ("ones", 128), ("hmask", 4), ("bd_gla", 256), ("bd_ret", 128), ("gL", 2)]:
    CST[_n] = (_o, _w)
    _o += _w
for _d in ("f", "b"):
    for _n, _w in [("M", 128), ("Mc", 128), ("nM", 128), ("mb", 1024), ("m01", 512), ("rdec", 512),
                   ("rgT", 256), ("rkd", 256)]:
        CST[_n + _d] = (_o, _w)
        _o += _w
NCST = _o


def _arrange_w_in(w):
    z = np.zeros((w.shape[0], 1), np.float32)
    zc = lambda n: np.zeros((w.shape[0], n), np.float32)
    cols = [w[:, 512:1536], w[:, 1552:1680], w[:, 1680:1808],
            w[:, 2320:2336], zc(16), w[:, 2336:2352], zc(80),
            w[:, 0:512],
            w[:, 1680:1808], w[:, 1808:2064], w[:, 1536:1552],
            w[:, 2064:2320], w[:, 3120:3376],
            w[:, 2352:2608], w[:, 2608:2864],
            w[:, 2864:3120]]
    out = np.concatenate(cols, axis=1)
    assert out.shape[1] == NA
    return np.ascontiguousarray(out)


def _build_consts(S):
    c = np.zeros((128, NCST), np.float32)

    def put(name, arr):
        o, w = CST[name]
        c[:, o:o + w] = np.asarray(arr, np.float32).reshape(128, w)

    r = np.arange(128)
    put("ident", np.eye(128))
    put("ones", np.ones((128, 128)))
    hm = np.zeros((128, 4)); hm[r, r // 32] = 1.0
    put("hmask", hm)
    put("bd_gla", (r[:, None] // 32 == (np.arange(256)[None, :] // 64)).astype(np.float32))
    put("bd_ret", (r[:, None] // 64 == (np.arange(128)[None, :] // 64)).astype(np.float32))
    gam = 1.0 - 2.0 ** (-5.0 - np.arange(4, dtype=np.float64))
    gl = np.zeros((128, 2))
    for t in range(2):
        gl[:, t] = gam[2 * t + r // 64] ** L
    put("gL", gl)
    s_ = r[:, None].astype(np.float64)
    t_ = r[None, :].astype(np.float64)
    for d in ("f", "b"):
        if d == "f":
            M = (s_ <= t_).astype(np.float64)
            allow = (s_ <= t_)
            dist = t_ - s_
            pq = t_ + 1.0
            ks = (L - 1.0) - r.astype(np.float64)
        else:
            M = (s_ >= t_).astype(np.float64)
            allow = (s_ > t_)
            dist = s_ - t_
            pq = L - t_
            ks = r.astype(np.float64)
        put("M" + d, M)
        put("Mc" + d, 1.0 - M)
        put("nM" + d, -M)
        mb = np.where(allow, 0.0, NEG)
        put("mb" + d, np.broadcast_to(mb[:, None, :], (128, 8, 128)))
        put("m01" + d, np.broadcast_to(allow.astype(np.float64)[:, None, :], (128, 4, 128)))
        rdec = np.zeros((128, 4, 128))
        for h in range(4):
            rdec[:, h, :] = np.where(allow, gam[h] ** np.maximum(dist, 0.0), 0.0) * 0.125
        put("rdec" + d, rdec)
        rg = np.zeros((128, 2, 128))
        for t in range(2):
            gh = gam[2 * t + r // 64]
            rg[:, t, :] = gh[:, None] ** np.broadcast_to(pq, (128, 128))
        put("rgT" + d, rg)
        rk = np.zeros((128, 256))
        for h in range(4):
            rk[:, h * 64:(h + 1) * 64] = (gam[h] ** ks)[:, None] * 0.125
        put("rkd" + d, rk)
    half = 32
    inv = (10000.0 ** (-np.arange(half, dtype=np.float32) / half)).astype(np.float32)
    ang = np.arange(S, dtype=np.float32)[:, None] * inv[None, :]
    rope = np.concatenate([np.cos(ang), np.sin(ang)], axis=1).astype(np.float32)
    return c, np.ascontiguousarray(rope)


class Res:
    __slots__ = ("name", "w", "rd", "const")

    def __init__(self, name, const=False):
        self.name = name
        self.w = None
        self.rd = {}
        self.const = const


class Tile:
    def __init__(self, h, name):
        self.h = h
        self.r = Res(name)

    def __getitem__(self, k):
        return self.h[k]


class Prog:
    ENG = ("pe", "act", "dve", "pool")

    NSET = 4

    def __init__(self, nc, es):
        self.nc = nc
        self.es = es
        self.e = {"pe": nc.tensor, "act": nc.scalar, "dve": nc.vector, "pool": nc.gpsimd, "sp": nc.sync}
        self.dnames = ("ldf0", "ldf1", "ldt0", "ldt1", "st0", "st1")
        self.eh = {n: [es.enter_context(nc.semaphore("s%d_%s" % (i, n))) for i in range(self.NSET)] for n in self.ENG}
        self.ec = {n: [0] * self.NSET for n in self.ENG}
        self.dh = {n: [es.enter_context(nc.semaphore("d%d_%s" % (i, n))) for i in range(self.NSET)] for n in self.dnames}
        self.dc = {n: [0] * self.NSET for n in self.dnames}
        self.si = 0
        self.sem = {n: self.eh[n][0] for n in self.ENG}
        self.cnt = {n: 0 for n in self.ENG}
        self.dsem = {n: self.dh[n][0] for n in self.dnames}
        self.dsem["w"] = es.enter_context(nc.semaphore("d_w"))
        self.dcnt = {n: 0 for n in self.dsem}
        self.epoch = {}
        for n in self.ENG:
            self.epoch["e" + n] = 0
        for n in self.dsem:
            self.epoch["d" + n] = 0
        self.waited = {n: {} for n in ("pe", "act", "dve", "pool", "sp")}
        self.n_inst = 0
        self.limit = int(os.environ.get('OP_LIMIT', '100000000'))
        self.log = os.environ.get('OP_LOG')

    def _bump(self, key):
        self.epoch[key] += 1
        for w in self.waited.values():
            w.pop(key, None)

    def switch_pe_sem(self):
        i = self.si
        for n in self.ENG:
            self.ec[n][i] = self.cnt[n]
        for n in self.dnames:
            self.dc[n][i] = self.dcnt[n]
        i = (i + 1) % self.NSET
        self.si = i
        for n in self.ENG:
            self.sem[n] = self.eh[n][i]
            self.cnt[n] = self.ec[n][i]
            self._bump("e" + n)
        for n in self.dnames:
            self.dsem[n] = self.dh[n][i]
            self.dcnt[n] = self.dc[n][i]
            self._bump("d" + n)

    def switch_dma_set(self, si):
        pass

    def _wait(self, eng, tok):
        kind, name, val, ep = tok
        if ep != self.epoch[kind + name]:
            return
        if kind == "e":
            if name == eng and eng == "pe":
                return
            sem = self.sem[name]
        else:
            sem = self.dsem[name]
            val = self.dcnt[name]
        key = kind + name
        if self.waited[eng].get(key, -1) >= val:
            return
        self.e[eng].wait_ge(sem, val)
        self.waited[eng][key] = val

    def _deps(self, eng, outs, ins):
        for t in ins:
            r = t.r
            if r.w is not None:
                self._wait(eng, r.w)
        for t in outs:
            r = t.r
            if r.w is not None:
                self._wait(eng, r.w)
            for tok in r.rd.values():
                self._wait(eng, tok)

    def _mark(self, tok, outs, ins):
        for t in ins:
            if not t.r.const:
                t.r.rd[tok[0] + tok[1]] = tok
        for t in outs:
            t.r.w = tok
            t.r.rd = {}

    def op(self, eng, fn, outs=(), ins=()):
        if self.n_inst >= self.limit:
            return
        if self.log:
            import sys
            print("OP", self.n_inst, eng, sys._getframe(1).f_lineno)
        self._deps(eng, outs, ins)
        inst = fn(self.e[eng])
        self.cnt[eng] += 1
        inst.then_inc(self.sem[eng], 1)
        self._mark(("e", eng, self.cnt[eng], self.epoch["e" + eng]), outs, ins)
        self.n_inst += 1

    def dma(self, semname, out, in_, outs=(), ins=(), q="sp", first=True):
        if self.n_inst >= self.limit:
            return
        if first and self.dcnt[semname] > 0:
            self._wait(q, ("d", semname, self.dcnt[semname], self.epoch["d" + semname]))
        if self.log:
            import sys
            print("DMA", self.n_inst, semname, sys._getframe(1).f_lineno)
        self._deps(q, outs, ins)
        inst = self.e[q].dma_start(out=out, in_=in_)
        self.dcnt[semname] += 16
        inst.then_inc(self.dsem[semname], 16)
        self._mark(("d", semname, self.dcnt[semname], self.epoch["d" + semname]), outs, ins)
        self.n_inst += 1

    def barrier(self):
        for eng in ("pe", "act", "dve", "pool", "sp"):
            for n in self.ENG:
                if n != eng and self.cnt[n] > 0:
                    self._wait(eng, ("e", n, self.cnt[n], self.epoch["e" + n]))
            for n in self.dsem:
                if self.dcnt[n] > 0:
                    self._wait(eng, ("d", n, self.dcnt[n], self.epoch["d" + n]))


class Ctx:
    pass


_UID = [0]


def _alloc(nc, es, name, shape, dt, psum=False):
    _UID[0] += 1
    name = "%s_%d" % (name, _UID[0])
    if psum:
        h = es.enter_context(nc.psum_tensor(name, shape, dt))
    else:
        h = es.enter_context(nc.sbuf_tensor(name, shape, dt))
    return Tile(h, name)


def bc(ap, shape, axis):
    return ap.unsqueeze(axis).to_broadcast(shape)


def phase_p1(P, nc, g, layer, h_src):
    S = g.S
    nsc = S // SC
    with contextlib.ExitStack() as es:
        A = lambda n, s, d, ps=False: _alloc(nc, es, n, s, d, ps)
        wb = A("p1_wb", [128, 8, NA], BF16)
        wst = [A("p1_wst%d" % i, [128, NA], F32) for i in range(2)]
        idf = A("p1_idf", [128, 128], F32)
        idb = A("p1_idb", [128, 128], BF16)
        hin = [A("p1_hin%d" % i, [128, 1024], F32) for i in range(2)]
        cs = [A("p1_cs%d" % i, [128, 4, 64], F32) for i in range(2)]
        hb = A("p1_hb", [128, 1024], BF16)
        hT = A("p1_hT", [128, 8, SC], BF16)
        stf = [A("p1_stf%d" % i, [128, SC], F32) for i in range(2)]
        stt = [A("p1_stt%d" % i, [128, 512], F32) for i in range(2)]
        rot = [A("p1_rot%d" % i, [128, 512], F32) for i in range(2)]
        ra = A("p1_ra", [128, 512], F32)
        rb_ = A("p1_rb", [128, 512], F32)
        rotb = A("p1_rotb", [128, 512], BF16)
        rqk = [A("p1_rqk%d" % i, [128, 4, SC], BF16) for i in range(2)]
        pT = A("p1_pT", [128, 1024], BF16, True)
        pacc = [A("p1_pacc%d" % i, [128, 512], F32, True) for i in range(4)]
        pR = A("p1_pR", [128, 1024], BF16, True)
        idf.r.const = True
        idb.r.const = True
        wb.r.const = True

        o, w = CST["ident"]
        P.dma("w", idf[:], g.cst[:, o:o + w], outs=[idf])
        P.op("pool", lambda e: e.tensor_copy(out=idb[:], in_=idf[:]), outs=[idb], ins=[idf])
        for k in range(8):
            ws = wst[k % 2]
            P.dma("w", ws[:], g.w_in[layer, k * 128:(k + 1) * 128, :], outs=[ws])
            eng = ("act", "dve")[k % 2]
            if eng == "act":
                P.op("act", lambda e, k=k, ws=ws: e.copy(out=wb[:, k, :], in_=ws[:]), outs=[wb], ins=[ws])
            else:
                P.op(eng, lambda e, k=k, ws=ws: e.tensor_copy(out=wb[:, k, :], in_=ws[:]), outs=[wb], ins=[ws])

        ev = [0]

        def evac(out_ap, in_ap, outs, ins):
            ev[0] += 1
            if ev[0] % 2:
                P.op("act", lambda e: e.copy(out=out_ap, in_=in_ap), outs=outs, ins=ins)
            else:
                P.op("dve", lambda e: e.tensor_copy(out=out_ap, in_=in_ap), outs=outs, ins=ins)

        it = 0
        sti = 0
        for j in range(nsc):
            t0 = j * SC
            P.dma("ldf%d" % (j % 2), cs[j % 2][:], g.rope[t0:t0 + SC, :].rearrange("(c p) f -> p c f", p=128), outs=[cs[j % 2]])
            for c in range(4):
                sl = it % 2
                it += 1
                r0 = t0 + c * 128
                P.dma("ldt%d" % sl, hin[sl][:], h_src[r0:r0 + 128, :], outs=[hin[sl]])
                P.op("dve", lambda e, sl=sl: e.tensor_copy(out=hb[:], in_=hin[sl][:]), outs=[hb], ins=[hin[sl]])
                for k in range(8):
                    P.op("pe", lambda e, k=k: e.transpose(pT[:, k * 128:(k + 1) * 128], hb[:, k * 128:(k + 1) * 128], idb[:]),
                         outs=[pT], ins=[hb, idb])
                evac(hT[:, :, c * 128:(c + 1) * 128], pT[:].rearrange("p (k t) -> p k t", k=8), [hT], [pT])
            for n in range(N_FM):
                pa = pacc[n % 4]
                for k in range(8):
                    P.op("pe", lambda e, k=k, n=n, pa=pa: e.matmul(pa[:, :], lhsT=wb[:, k, n * 128:(n + 1) * 128], rhs=hT[:, k, :],
                                                                   start=(k == 0), stop=(k == 7)), outs=[pa], ins=[wb, hT])
                sf = stf[sti % 2]
                ssem = "st%d" % (sti % 2)
                sti += 1
                evac(sf[:], pa[:, :], [sf], [pa])
                if n < 8:
                    dst = g.xbct[n * 128:(n + 1) * 128, 2 + t0:2 + t0 + SC]
                elif n == 8:
                    dst = g.gqt[:, t0:t0 + SC]
                elif n == 9:
                    dst = g.gkt[:, t0:t0 + SC]
                else:
                    dst = g.alrt[:, t0:t0 + SC]
                P.dma(ssem, dst, sf[:], ins=[sf])
            rq = rqk[j % 2]
            for c in range(4):
                r0 = t0 + c * 128
                for gi, (gn, gw) in enumerate(TM_GROUPS):
                    pa = pacc[gi % 4]
                    off = TM_OFF[gn]
                    for k in range(8):
                        P.op("pe", lambda e, k=k, pa=pa, off=off, gw=gw, c=c: e.matmul(
                            pa[:, 0:gw], lhsT=hT[:, k, c * 128:(c + 1) * 128], rhs=wb[:, k, off:off + gw],
                            start=(k == 0), stop=(k == 7)), outs=[pa], ins=[wb, hT])
                    if gn != "rqk":
                        sf = stt[sti % 2]
                        ssem = "st%d" % (sti % 2)
                        sti += 1
                        evac(sf[:, 0:gw], pa[:, 0:gw], [sf], [pa])
                        dst = {"z": g.z_tm, "gkvd": g.gkvd_tm, "gate": g.gate_tm, "rv": g.rv_tm}[gn]
                        P.dma(ssem, dst[r0:r0 + 128, :], sf[:, 0:gw], ins=[sf])
                    else:
                        ro = rot[sti % 2]
                        ssem = "st%d" % (sti % 2)
                        sti += 1
                        csl = cs[j % 2]
                        R4 = pa[:, :].rearrange("p (h a f) -> p h a f", h=8, a=2)
                        cosb = csl[:, c, 0:32].unsqueeze(1).unsqueeze(1).to_broadcast([128, 8, 2, 32])
                        sinb = csl[:, c, 32:64].unsqueeze(1).unsqueeze(1).to_broadcast([128, 8, 2, 32])
                        A4 = ra[:].rearrange("p (h a f) -> p h a f", h=8, a=2)
                        B4 = rb_[:].rearrange("p (h a f) -> p h a f", h=8, a=2)
                        O4 = ro[:].rearrange("p (h a f) -> p h a f", h=8, a=2)
                        P.op("dve", lambda e: e.tensor_tensor(out=A4, in0=R4, in1=cosb, op=ALU.mult), outs=[ra], ins=[pa, csl])
                        P.op("dve", lambda e: e.tensor_tensor(out=B4, in0=R4, in1=sinb, op=ALU.mult), outs=[rb_], ins=[pa, csl])
                        P.op("dve", lambda e: e.tensor_tensor(out=O4[:, :, 0, :], in0=A4[:, :, 0, :], in1=B4[:, :, 1, :], op=ALU.subtract),
                             outs=[ro], ins=[ra, rb_])
                        P.op("dve", lambda e: e.tensor_tensor(out=O4[:, :, 1, :], in0=A4[:, :, 1, :], in1=B4[:, :, 0, :], op=ALU.add),
                             outs=[ro], ins=[ra, rb_])
                        P.dma(ssem, g.rk_tm[r0:r0 + 128, :], ro[:, 256:512], ins=[ro])
                        P.op("act", lambda e: e.copy(out=rotb[:], in_=ro[:]), outs=[rotb], ins=[ro])
                        for n in range(4):
                            P.op("pe", lambda e, n=n: e.transpose(pR[:, n * 128:(n + 1) * 128], rotb[:, n * 128:(n + 1) * 128], idb[:]),
                                 outs=[pR], ins=[rotb, idb])
                        evac(rq[:, :, c * 128:(c + 1) * 128], pR[:, 0:512].rearrange("p (n t) -> p n t", n=4), [rq], [pR])
            P.dma("st%d" % (j % 2), g.rqkt.rearrange("(n p) t -> p n t", p=128)[:, :, t0:t0 + SC], rq[:], ins=[rq])
    P.barrier()


def phase_c(P, nc, g, layer):
    S = g.S
    nsc = S // SC
    with contextlib.ExitStack() as es:
        A = lambda n, s, d, ps=False: _alloc(nc, es, n, s, d, ps)
        idf = A("c_idf", [128, 128], F32)
        idb = A("c_idb", [128, 128], BF16)
        cw = A("c_cw", [128, 48], F32)
        xin = [A("c_xin%d" % i, [128, 8, SC + 4], F32) for i in range(2)]
        acc = [A("c_acc%d" % i, [128, SC], F32) for i in range(4)]
        xact = [A("c_xact%d" % i, [128, 4, SC], F32) for i in range(2)]
        bct = [A("c_bct%d" % i, [128, 4, SC], BF16) for i in range(2)]
        xtm = [A("c_xtm%d" % i, [128, 512], F32) for i in range(2)]
        btm = [A("c_btm%d" % i, [128, 256], BF16) for i in range(2)]
        pX = [A("c_pX%d" % i, [128, 512], F32, True) for i in range(2)]
        pB = [A("c_pB%d" % i, [128, 1024], BF16, True) for i in range(2)]
        idf.r.const = True
        idb.r.const = True
        cw.r.const = True
        o, w = CST["ident"]
        P.dma("w", idf[:], g.cst[:, o:o + w], outs=[idf])
        P.dma("w", cw[:], g.convp[layer], outs=[cw])
        P.op("pool", lambda e: e.tensor_copy(out=idb[:], in_=idf[:]), outs=[idb], ins=[idf])
        it = 0
        for j in range(nsc):
            t0 = j * SC
            xi = xin[j % 2]
            P.dma("ldf%d" % (j % 2), xi[:], g.xbct.rearrange("(n p) t -> p n t", p=128)[:, :, t0:t0 + SC + 4], outs=[xi])
            xa = xact[j % 2]
            bt = bct[j % 2]
            for n in range(8):
                ac = acc[n % 4]
                eng = "dve"
                P.op(eng, lambda e: e.tensor_scalar(out=ac[:], in0=xi[:, n, 0:SC], scalar1=cw[:, n * 5:n * 5 + 1],
                                                    scalar2=cw[:, 40 + n:41 + n], op0=ALU.mult, op1=ALU.add), outs=[ac], ins=[xi, cw])
                for k in range(1, 5):
                    P.op("dve", lambda e: e.scalar_tensor_tensor(out=ac[:], in0=xi[:, n, k:k + SC], scalar=cw[:, n * 5 + k:n * 5 + k + 1],
                                                                 in1=ac[:], op0=ALU.mult, op1=ALU.add), outs=[ac], ins=[xi, cw, ac])
                if n < 4:
                    P.op("act", lambda e: e.activation(out=xa[:, n, :], in_=ac[:], func=AF.Silu), outs=[xa], ins=[ac])
                else:
                    P.op("act", lambda e: e.activation(out=bt[:, n - 4, :], in_=ac[:], func=AF.Silu), outs=[bt], ins=[ac])
            P.dma("st%d" % (j % 2), g.bct.rearrange("(n p) t -> p n t", p=128)[:, :, t0:t0 + SC], bt[:], ins=[bt])
            for c in range(4):
                r0 = t0 + c * 128
                sl = it % 2
                it += 1
                px = pX[sl]
                pb = pB[sl]
                for n in range(4):
                    P.op("pe", lambda e: e.transpose(px[:, n * 128:(n + 1) * 128], xa[:, n, c * 128:(c + 1) * 128], idf[:]),
                         outs=[px], ins=[xa, idf])
                for n in range(2):
                    P.op("pe", lambda e: e.transpose(pb[:, n * 128:(n + 1) * 128], bt[:, n, c * 128:(c + 1) * 128], idb[:]),
                         outs=[pb], ins=[bt, idb])
                xt = xtm[sl]
                bm = btm[sl]
                P.op("act", lambda e: e.copy(out=xt[:], in_=px[:, :]), outs=[xt], ins=[px])
                P.op("dve", lambda e: e.tensor_copy(out=bm[:], in_=pb[:, 0:256]), outs=[bm], ins=[pb])
                P.dma("st%d" % sl, g.x_tm[r0:r0 + 128, :], xt[:], ins=[xt])
                P.dma("st%d" % sl, g.b_tm[r0:r0 + 128, :], bm[:], ins=[bm])
    P.barrier()


def phase_sweep(P, nc, g, layer, d):
    S = g.S
    nsc = S // SC
    fwd = (d == "f")
    di = 0 if fwd else 1
    with contextlib.ExitStack() as es:
        A = lambda n, s, dt, ps=False: _alloc(nc, es, "s_" + n, s, dt, ps)
        NG = CST["gL"][0] + CST["gL"][1]
        d0 = CST["M" + d][0]
        ND = CST["rkd" + d][0] + CST["rkd" + d][1] - d0
        cg = A("cg", [128, NG], F32)
        cd = A("cd", [128, ND], F32)
        vsm = A("vsm", [128, 808], F32)
        negA = A("negA", [128, 16], F32)
        wa2 = A("wa2", [48, 128], F32)
        idb = A("idb", [128, 128], BF16)
        for t in (cg, cd, vsm, negA, wa2, idb):
            t.r.const = True
        CG = lambda n: cg[:, CST[n][0]:CST[n][0] + CST[n][1]]
        CD = lambda n: cd[:, CST[n + d][0] - d0:CST[n + d][0] - d0 + CST[n + d][1]]
        VS = lambda n: vsm[:, VEC[n][0]:VEC[n][0] + VEC[n][1]]
        bct_s = [A("bct%d" % i, [128, 4, SC], BF16) for i in range(2)]
        gqk_s = [A("gqk%d" % i, [128, 2, SC], F32) for i in range(2)]
        alr_s = [A("alr%d" % i, [48, SC], F32) for i in range(2)]
        rqk_s = [A("rqk%d" % i, [128, 4, SC], BF16) for i in range(2)]
        btm_s = [A("btm%d" % i, [128, 256], BF16) for i in range(2)]
        xtm_s = [A("xtm%d" % i, [128, 512], F32) for i in range(2)]
        gkvd_s = [A("gkvd%d" % i, [128, 400], F32) for i in range(2)]
        rk_s = [A("rk%d" % i, [128, 256], F32) for i in range(2)]
        rv_s = [A("rv%d" % i, [128, 256], F32) for i in range(2)]
        yb_s = [A("yb%d" % i, [128, 1024], F32) for i in range(2)] if fwd else None
        ycat = [A("ycat%d" % i, [128, 1024], F32) for i in range(2)]
        S_ssd = A("S_ssd", [128, 512], F32)
        Sb_ssd = A("Sb_ssd", [128, 512], BF16)
        S_gla = A("S_gla", [128, 256], F32)
        Sb_gla = A("Sb_gla", [128, 256], BF16)
        S_ret = A("S_ret", [128, 2, 128], F32)
        Sb_ret = A("Sb_ret", [128, 2, 128], BF16)
        dtx = A("dtx", [128, 8], F32)
        dte = A("dte", [128, 8], F32)
        dtt = A("dtt", [128, 8], F32)
        la = A("la", [128, 8], F32)
        E3 = A("E3", [128, 24], F32)
        c2 = A("c2", [128, 8], F32)
        rhs1 = A("rhs1", [128, 8, 128], F32)
        rhs2 = A("rhs2", [128, 8, 128], F32)
        decT = A("decT", [128, 8, 128], F32)
        scT = A("scT", [128, 8, 128], BF16)
        Vt = A("Vt", [128, 512], BF16)
        Vs = A("Vs", [128, 512], BF16)
        tmpz = A("tmpz", [128, 512], F32)
        tmpd = A("tmpd", [128, 512], F32)
        xgb = A("xgb", [128, 128], F32)
        ge = A("ge", [128, 128], F32)
        gl = A("gl", [128, 128], F32)
        E1 = A("E1", [128, 128], F32)
        E2 = A("E2", [128, 128], F32)
        E2t = A("E2t", [128, 128], F32)
        qd = A("qd", [128, 128], BF16)
        kd = A("kd", [128, 128], BF16)
        kdtm = A("kdtm", [128, 128], BF16)
        qbd = A("qbd", [128, 4, 128], BF16)
        gscm = A("gscm", [128, 4, 128], BF16)
        GVb = A("GVb", [128, 256], BF16)
        gt1 = A("gt1", [128, 256], F32)
        rscm = A("rscm", [128, 4, 128], BF16)
        qdT = A("qdT", [128, 2, 128], BF16)
        rqbd = A("rqbd", [128, 2, 2, 128], BF16)
        rkdt = A("rkdt", [128, 256], BF16)
        RVb = A("RVb", [128, 256], BF16)
        rt1 = A("rt1", [128, 2, 128], F32)
        bank = [_alloc(nc, es, "s_bank%d" % i, [128, 512], F32, True).h for i in range(8)]
        pD = [Tile(bank[0], "pD0"), Tile(bank[1], "pD1")]
        pPP = Tile(bank[2], "pPP")
        pG = Tile(bank[2], "pG")
        pXG = Tile(bank[2], "pXG")
        pYi = Tile(bank[3], "pYi")
        pZS = Tile(bank[4], "pZS")
        pSC = Tile(bank[5], "pSC")
        pPT = Tile(bank[6], "pPT")
        pPtm = Tile(bank[6], "pPtm")
        pYg = Tile(bank[6], "pYg")
        pYr = Tile(bank[7], "pYr")
        pS2 = Tile(bank[7], "pS2")
        pG.r = pPP.r
        pXG.r = pPP.r
        pPtm.r = pPT.r
        pYg.r = pPT.r
        pS2.r = pYr.r

        P.dma("w", cg[:], g.cst[:, 0:NG], outs=[cg])
        P.dma("w", cd[:], g.cst[:, d0:d0 + ND], outs=[cd])
        P.dma("w", vsm[:], g.vec[layer, :, 0:808], outs=[vsm])
        P.dma("w", wa2[:], g.wa2[layer], outs=[wa2])
        P.op("pool", lambda e: e.tensor_copy(out=idb[:], in_=CG("ident")), outs=[idb], ins=[cg])
        P.op("act", lambda e: e.activation(out=negA[:], in_=VS("alog"), func=AF.Exp), outs=[negA], ins=[vsm])
        P.op("dve", lambda e: e.tensor_scalar(out=negA[:], in0=negA[:], scalar1=-1.0, scalar2=None, op0=ALU.mult), outs=[negA], ins=[negA])
        for st in (S_ssd, Sb_ssd, S_gla, Sb_gla, S_ret, Sb_ret):
            P.op("pool", lambda e: e.memset(st[:], 0.0), outs=[st])
        ident_f = CG("ident")
        ones_f = CG("ones")

        order_sc = list(range(nsc)) if fwd else list(range(nsc - 1, -1, -1))
        order_c = list(range(4)) if fwd else [3, 2, 1, 0]
        it = 0
        for ji, j in enumerate(order_sc):
            t0 = j * SC
            fs = ji % 2
            fsem = "ldf%d" % fs
            P.dma(fsem, bct_s[fs][:], g.bct.rearrange("(n p) t -> p n t", p=128)[:, :, t0:t0 + SC], outs=[bct_s[fs]])
            P.dma(fsem, gqk_s[fs][:, 0, :], g.gqt[:, t0:t0 + SC], outs=[gqk_s[fs]], first=False)
            P.dma(fsem, gqk_s[fs][:, 1, :], g.gkt[:, t0:t0 + SC], outs=[gqk_s[fs]], first=False)
            P.dma(fsem, alr_s[fs][:], g.alrt[0:48, t0:t0 + SC], outs=[alr_s[fs]], first=False)
            P.dma(fsem, rqk_s[fs][:], g.rqkt.rearrange("(n p) t -> p n t", p=128)[:, :, t0:t0 + SC], outs=[rqk_s[fs]], first=False)
            for c in order_c:
                r0 = t0 + c * 128
                ts = it % 2
                it += 1
                tsem = "ldt%d" % ts
                btm, xtm, gkvd, rkt, rvt, yc = btm_s[ts], xtm_s[ts], gkvd_s[ts], rk_s[ts], rv_s[ts], ycat[ts]
                P.dma(tsem, btm[:], g.b_tm[r0:r0 + 128, :], outs=[btm])
                P.dma(tsem, xtm[:], g.x_tm[r0:r0 + 128, :], outs=[xtm], first=False)
                P.dma(tsem, gkvd[:], g.gkvd_tm[r0:r0 + 128, :], outs=[gkvd], first=False)
                P.dma(tsem, rkt[:], g.rk_tm[r0:r0 + 128, :], outs=[rkt], first=False)
                P.dma(tsem, rvt[:], g.rv_tm[r0:r0 + 128, :], outs=[rvt], first=False)
                if fwd:
                    P.dma(tsem, yb_s[ts][:], g.yb[r0:r0 + 128, :], outs=[yb_s[ts]], first=False)
                csl = slice(c * 128, (c + 1) * 128)
                BC = bct_s[fs]
                PARTS = os.environ.get('SWEEP_PARTS', 'ssd,gla,ret').split(',')
                if 'ssd' not in PARTS:
                    P.op('pool', lambda e: e.memset(yc[:, 0:512], 0.0), outs=[yc])
                if 'ssd' in PARTS:
                    P.op("dve", lambda e: e.tensor_tensor(out=dtx[:], in0=gkvd[:, 384 + di * 8:392 + di * 8],
                                                          in1=VS("dtb")[:, di * 8:di * 8 + 8], op=ALU.add), outs=[dtx], ins=[gkvd, vsm])
                    P.op("act", lambda e: e.activation(out=dte[:], in_=dtx[:], func=AF.Exp), outs=[dte], ins=[dtx])
                    P.op("act", lambda e: e.activation(out=dtt[:], in_=dte[:], func=AF.Ln, bias=1.0), outs=[dtt], ins=[dte])
                    P.op("dve", lambda e: e.tensor_tensor(out=la[:], in0=dtt[:], in1=negA[:, di * 8:di * 8 + 8], op=ALU.mult),
                         outs=[la], ins=[dtt, negA])
                    P.op("pe", lambda e: e.matmul(pPP[:, 0:8], lhsT=CD("M"), rhs=la[:], start=True, stop=True), outs=[pPP], ins=[cd, la])
                    P.op("pe", lambda e: e.matmul(pPP[:, 8:16], lhsT=CD("Mc"), rhs=la[:], start=True, stop=True), outs=[pPP], ins=[cd, la])
                    P.op("pe", lambda e: e.matmul(pPP[:, 16:24], lhsT=ones_f, rhs=la[:], start=True, stop=True), outs=[pPP], ins=[cg, la])
                    P.op("act", lambda e: e.activation(out=E3[:], in_=pPP[:, 0:24], func=AF.Exp), outs=[E3], ins=[pPP])
                    P.op("pool", lambda e: e.tensor_tensor(out=rhs1[:], in0=bc(la[:], [128, 8, 128], 2),
                                                           in1=bc(CD("M"), [128, 8, 128], 1), op=ALU.mult), outs=[rhs1], ins=[la, cd])
                    P.op("pool", lambda e: e.tensor_copy(out=rhs2[:], in_=bc(la[:], [128, 8, 128], 2)), outs=[rhs2], ins=[la])
                    mbv = CD("mb").rearrange("p (h t) -> p h t", h=8)
                    for hf in range(2):
                        hs = slice(hf * 4, hf * 4 + 4)
                        P.op("pe", lambda e: e.matmul(pD[hf][:, :], lhsT=ones_f, rhs=rhs1[:, hs, :], start=True, stop=False), outs=[pD[hf]], ins=[cg, rhs1])
                        P.op("pe", lambda e: e.matmul(pD[hf][:, :], lhsT=CD("nM"), rhs=rhs2[:, hs, :], start=False, stop=False), outs=[pD[hf]], ins=[cd, rhs2])
                        P.op("pe", lambda e: e.matmul(pD[hf][:, :], lhsT=ident_f, rhs=mbv[:, hs, :], start=False, stop=True), outs=[pD[hf]], ins=[cg, cd])
                        P.op("act", lambda e: e.activation(out=decT[:, hs, :], in_=pD[hf][:, :].rearrange("p (h t) -> p h t", h=4), func=AF.Exp),
                             outs=[decT], ins=[pD[hf]])
                    for gi in range(2):
                        P.op("pe", lambda e: e.matmul(pG[:, 128 + gi * 128:256 + gi * 128], lhsT=BC[:, gi, csl], rhs=BC[:, 2 + gi, csl],
                                                      start=True, stop=True), outs=[pG], ins=[BC])
                    Gv = pG[:, 128:384].rearrange("p (g t) -> p g t", g=2)
                    for gi in range(2):
                        P.op("dve", lambda e: e.tensor_tensor(out=scT[:, gi * 4:gi * 4 + 4, :], in0=bc(Gv[:, gi, :], [128, 4, 128], 1),
                                                              in1=decT[:, gi * 4:gi * 4 + 4, :], op=ALU.mult), outs=[scT], ins=[pG, decT])
                    P.op("dve", lambda e: e.tensor_tensor(out=c2[:], in0=dtt[:], in1=E3[:, 8:16], op=ALU.mult), outs=[c2], ins=[dtt, E3])
                    X3 = xtm[:].rearrange("p (h f) -> p h f", h=8)
                    P.op("pool", lambda e: e.tensor_tensor(out=Vt[:].rearrange("p (h f) -> p h f", h=8), in0=X3, in1=bc(dtt[:], [128, 8, 64], 2), op=ALU.mult),
                         outs=[Vt], ins=[xtm, dtt])
                    P.op("pool", lambda e: e.tensor_tensor(out=Vs[:].rearrange("p (h f) -> p h f", h=8), in0=X3, in1=bc(c2[:], [128, 8, 64], 2), op=ALU.mult),
                         outs=[Vs], ins=[xtm, c2])
                    for h in range(8):
                        P.op("pe", lambda e: e.matmul(pYi[:, h * 64:(h + 1) * 64], lhsT=scT[:, h, :], rhs=Vt[:, h * 64:(h + 1) * 64], start=True, stop=True),
                             outs=[pYi], ins=[scT, Vt])
                    for gi in range(2):
                        P.op("pe", lambda e: e.matmul(pZS[:, gi * 256:(gi + 1) * 256], lhsT=BC[:, 2 + gi, csl], rhs=Sb_ssd[:, gi * 256:(gi + 1) * 256],
                                                      start=True, stop=True), outs=[pZS], ins=[BC, Sb_ssd])
                    P.op("dve", lambda e: e.tensor_tensor(out=tmpz[:].rearrange("p (h f) -> p h f", h=8), in0=pZS[:, :].rearrange("p (h f) -> p h f", h=8),
                                                          in1=bc(E3[:, 0:8], [128, 8, 64], 2), op=ALU.mult), outs=[tmpz], ins=[pZS, E3])
                    P.op("dve", lambda e: e.tensor_tensor(out=yc[:, 0:512], in0=pYi[:, :], in1=tmpz[:], op=ALU.add), outs=[yc], ins=[pYi, tmpz])
                    if fwd:
                        P.op("pool", lambda e: e.tensor_tensor(out=tmpd[:].rearrange("p (h f) -> p h f", h=8), in0=X3, in1=bc(VS("dsk"), [128, 8, 64], 2), op=ALU.mult),
                             outs=[tmpd], ins=[xtm, vsm])
                        P.op("pool", lambda e: e.tensor_tensor(out=yc[:, 0:512], in0=yc[:, 0:512], in1=tmpd[:], op=ALU.add), outs=[yc], ins=[yc, tmpd])
                    for gi in range(2):
                        P.op("pe", lambda e: e.matmul(pZS[:, gi * 256:(gi + 1) * 256], lhsT=btm[:, gi * 128:(gi + 1) * 128], rhs=Vs[:, gi * 256:(gi + 1) * 256],
                                                      start=True, stop=True), outs=[pZS], ins=[btm, Vs])
                    P.op("pool", lambda e: e.tensor_tensor(out=S_ssd[:].rearrange("p (h f) -> p h f", h=8), in0=S_ssd[:].rearrange("p (h f) -> p h f", h=8),
                                                           in1=bc(E3[:, 16:24], [128, 8, 64], 2), op=ALU.mult), outs=[S_ssd], ins=[S_ssd, E3])
                    P.op("dve", lambda e: e.tensor_tensor(out=S_ssd[:], in0=S_ssd[:], in1=pZS[:, :], op=ALU.add), outs=[S_ssd], ins=[S_ssd, pZS])
                    P.op("act", lambda e: e.copy(out=Sb_ssd[:], in_=S_ssd[:]), outs=[Sb_ssd], ins=[S_ssd])
                if 'gla' not in PARTS:
                    P.op('pool', lambda e: e.memset(yc[:, 512:768], 0.0), outs=[yc])
                if 'gla' in PARTS:
                    AL = alr_s[fs]
                    GQK = gqk_s[fs]
                    P.op("pe", lambda e: e.matmul(pXG[:, 384:512], lhsT=AL[di * 32:di * 32 + 16, csl], rhs=wa2[di * 32:di * 32 + 16, :], start=True, stop=True),
                         outs=[pXG], ins=[AL, wa2])
                    P.op("dve", lambda e: e.tensor_tensor(out=xgb[:], in0=pXG[:, 384:512], in1=VS("ba")[:, di * 128:(di + 1) * 128], op=ALU.add),
                         outs=[xgb], ins=[pXG, vsm])
                    P.op("act", lambda e: e.activation(out=ge[:], in_=xgb[:], func=AF.Exp, scale=-1.0), outs=[ge], ins=[xgb])
                    P.op("act", lambda e: e.activation(out=gl[:], in_=ge[:], func=AF.Ln, bias=1.0), outs=[gl], ins=[ge])
                    P.op("pe", lambda e: e.matmul(pPT[:, 0:128], lhsT=gl[:], rhs=CD("M"), start=True, stop=True), outs=[pPT], ins=[gl, cd])
                    P.op("pe", lambda e: e.matmul(pPtm[:, 128:256], lhsT=CD("M"), rhs=gl[:], start=True, stop=True), outs=[pPtm], ins=[gl, cd])
                    P.op("act", lambda e: e.activation(out=E1[:], in_=pPT[:, 0:128], func=AF.Exp, scale=-1.0 / 16.0), outs=[E1], ins=[pPT])
                    P.op("act", lambda e: e.activation(out=E2[:], in_=pPT[:, 0:128], func=AF.Exp, scale=1.0 / 16.0), outs=[E2], ins=[pPT])
                    P.op("act", lambda e: e.activation(out=E2t[:], in_=pPtm[:, 128:256], func=AF.Exp, scale=1.0 / 16.0), outs=[E2t], ins=[pPtm])
                    P.op("dve", lambda e: e.scalar_tensor_tensor(out=qd[:], in0=GQK[:, 0, csl], scalar=float(32 ** -0.5), in1=E1[:], op0=ALU.mult, op1=ALU.mult),
                         outs=[qd], ins=[GQK, E1])
                    P.op("pool", lambda e: e.tensor_tensor(out=kd[:], in0=GQK[:, 1, csl], in1=E2[:], op=ALU.mult), outs=[kd], ins=[GQK, E2])
                    P.op("pool", lambda e: e.tensor_tensor(out=kdtm[:], in0=gkvd[:, 0:128], in1=E2t[:], op=ALU.mult), outs=[kdtm], ins=[gkvd, E2t])
                    P.op("dve", lambda e: e.tensor_tensor(out=qbd[:], in0=bc(qd[:], [128, 4, 128], 1), in1=bc(CG("hmask"), [128, 4, 128], 2), op=ALU.mult),
                         outs=[qbd], ins=[qd, cg])
                    P.op("act", lambda e: e.copy(out=GVb[:], in_=gkvd[:, 128:384]), outs=[GVb], ins=[gkvd])
                    P.op("pe", lambda e: e.matmul(pSC[:, :], lhsT=kd[:], rhs=qbd[:], start=True, stop=True), outs=[pSC], ins=[kd, qbd])
                    P.op("dve", lambda e: e.tensor_tensor(out=gscm[:], in0=pSC[:, :].rearrange("p (h t) -> p h t", h=4),
                                                          in1=CD("m01").rearrange("p (h t) -> p h t", h=4), op=ALU.mult), outs=[gscm], ins=[pSC, cd])
                    for h in range(4):
                        P.op("pe", lambda e: e.matmul(pYg[:, 256 + h * 64:256 + (h + 1) * 64], lhsT=qd[:], rhs=Sb_gla[:, h * 64:(h + 1) * 64], start=True, stop=False),
                             outs=[pYg], ins=[qd, Sb_gla])
                        P.op("pe", lambda e: e.matmul(pYg[:, 256 + h * 64:256 + (h + 1) * 64], lhsT=gscm[:, h, :], rhs=GVb[:, h * 64:(h + 1) * 64], start=False, stop=True),
                             outs=[pYg], ins=[gscm, GVb])
                    P.op("act", lambda e: e.copy(out=yc[:, 512:768], in_=pYg[:, 256:512]), outs=[yc], ins=[pYg])
                    eT = E1[:, 127:128] if fwd else E1[:, 0:1]
                    P.op("pe", lambda e: e.matmul(pS2[:, 256:512], lhsT=kdtm[:], rhs=GVb[:], start=True, stop=True), outs=[pS2], ins=[kdtm, GVb])
                    P.op("dve", lambda e: e.scalar_tensor_tensor(out=gt1[:], in0=pS2[:, 256:512], scalar=eT, in1=CG("bd_gla"), op0=ALU.mult, op1=ALU.mult),
                         outs=[gt1], ins=[pS2, E1, cg])
                    P.op("dve", lambda e: e.scalar_tensor_tensor(out=S_gla[:], in0=S_gla[:], scalar=eT, in1=gt1[:], op0=ALU.mult, op1=ALU.add),
                         outs=[S_gla], ins=[S_gla, E1, gt1])
                    P.op("pool", lambda e: e.tensor_copy(out=Sb_gla[:], in_=S_gla[:]), outs=[Sb_gla], ins=[S_gla])
                if 'ret' not in PARTS:
                    P.op('pool', lambda e: e.memset(yc[:, 768:1024], 0.0), outs=[yc])
                if 'ret' in PARTS:
                    RQ = rqk_s[fs]
                    hm2 = CG("bd_ret").rearrange("p (hh v) -> p hh v", hh=2)[:, :, 0]
                    P.op("dve", lambda e: e.tensor_tensor(out=rqbd[:], in0=RQ[:, 0:2, csl].unsqueeze(2).to_broadcast([128, 2, 2, 128]),
                                                          in1=hm2.unsqueeze(1).unsqueeze(3).to_broadcast([128, 2, 2, 128]), op=ALU.mult),
                         outs=[rqbd], ins=[RQ, cg])
                    for tl in range(2):
                        P.op("pe", lambda e: e.matmul(pSC[:, tl * 256:(tl + 1) * 256], lhsT=RQ[:, 2 + tl, csl], rhs=rqbd[:, tl, :, :], start=True, stop=True),
                             outs=[pSC], ins=[RQ, rqbd])
                    P.op("dve", lambda e: e.tensor_tensor(out=rscm[:], in0=pSC[:, :].rearrange("p (h t) -> p h t", h=4),
                                                          in1=CD("rdec").rearrange("p (h t) -> p h t", h=4), op=ALU.mult), outs=[rscm], ins=[pSC, cd])
                    P.op("dve", lambda e: e.tensor_tensor(out=qdT[:], in0=RQ[:, 0:2, csl], in1=CD("rgT").rearrange("p (n t) -> p n t", n=2), op=ALU.mult),
                         outs=[qdT], ins=[RQ, cd])
                    P.op("pool", lambda e: e.tensor_tensor(out=rkdt[:], in0=rkt[:], in1=CD("rkd"), op=ALU.mult), outs=[rkdt], ins=[rkt, cd])
                    P.op("act", lambda e: e.copy(out=RVb[:], in_=rvt[:]), outs=[RVb], ins=[rvt])
                    for h in range(4):
                        tl = h // 2
                        hc = slice((h % 2) * 64, (h % 2) * 64 + 64)
                        P.op("pe", lambda e: e.matmul(pYr[:, h * 64:(h + 1) * 64], lhsT=qdT[:, tl, :], rhs=Sb_ret[:, tl, hc], start=True, stop=False),
                             outs=[pYr], ins=[qdT, Sb_ret])
                        P.op("pe", lambda e: e.matmul(pYr[:, h * 64:(h + 1) * 64], lhsT=rscm[:, h, :], rhs=RVb[:, h * 64:(h + 1) * 64], start=False, stop=True),
                             outs=[pYr], ins=[rscm, RVb])
                    P.op("act", lambda e: e.copy(out=yc[:, 768:1024], in_=pYr[:, 0:256]), outs=[yc], ins=[pYr])
                    for tl in range(2):
                        P.op("pe", lambda e: e.matmul(pS2[:, 256 + tl * 128:256 + (tl + 1) * 128], lhsT=rkdt[:, tl * 128:(tl + 1) * 128],
                                                      rhs=RVb[:, tl * 128:(tl + 1) * 128], start=True, stop=True), outs=[pS2], ins=[rkdt, RVb])
                    P.op("dve", lambda e: e.tensor_tensor(out=rt1[:], in0=pS2[:, 256:512].rearrange("p (n t) -> p n t", n=2),
                                                          in1=bc(CG("bd_ret"), [128, 2, 128], 1), op=ALU.mult), outs=[rt1], ins=[pS2, cg])
                    for tl in range(2):
                        P.op("dve", lambda e: e.scalar_tensor_tensor(out=S_ret[:, tl, :], in0=S_ret[:, tl, :], scalar=CG("gL")[:, tl:tl + 1], in1=rt1[:, tl, :],
                                                                     op0=ALU.mult, op1=ALU.add), outs=[S_ret], ins=[S_ret, cg, rt1])
                    P.op("pool", lambda e: e.tensor_copy(out=Sb_ret[:], in_=S_ret[:]), outs=[Sb_ret], ins=[S_ret])
                if fwd:
                    P.op("dve", lambda e: e.tensor_tensor(out=yc[:], in0=yc[:], in1=yb_s[ts][:], op=ALU.add), outs=[yc], ins=[yc, yb_s[ts]])
                    P.dma("st%d" % ts, g.ys[r0:r0 + 128, :], yc[:], ins=[yc])
                else:
                    P.dma("st%d" % ts, g.yb[r0:r0 + 128, :], yc[:], ins=[yc])
    P.barrier()


def phase_o(P, nc, g, layer, h_src, h_dst):
    S = g.S
    nch = S // 128
    with contextlib.ExitStack() as es:
        A = lambda n, s, dt, ps=False: _alloc(nc, es, "o_" + n, s, dt, ps)
        idf = A("idf", [128, 128], F32)
        idb = A("idb", [128, 128], BF16)
        vec = A("vec", [128, NV], F32)
        wst = [A("wst%d" % i, [128, 1024], F32) for i in range(2)]
        wo = A("wo", [128, 8, 1024], BF16)
        wg = A("wg", [128, 8, 1024], BF16)
        wp = A("wp", [128, 2, 1024], BF16)
        for t in (idf, idb, vec, wo, wg, wp):
            t.r.const = True
        VS = lambda n: vec[:, VEC[n][0]:VEC[n][0] + VEC[n][1]]
        ys_s = [A("ys%d" % i, [128, 1024], F32) for i in range(2)]
        z_s = [A("z%d" % i, [128, 512], F32) for i in range(2)]
        gt_s = [A("gt%d" % i, [128, 512], F32) for i in range(2)]
        h_s = [A("h%d" % i, [128, 1024], F32) for i in range(2)]
        p_s = [A("p%d" % i, [128, 256], F32) for i in range(2)]
        ho_s = [A("ho%d" % i, [128, 1024], F32) for i in range(2)]
        gz = A("gz", [128, 512], F32)
        yg = A("yg", [128, 512], F32)
        sq = A("sq", [128, 1024], F32)
        st = A("st", [128, 32], F32)
        ycb = A("ycb", [128, 1024], BF16)
        ycT = A("ycT", [128, 8, 128], BF16)
        on = A("on", [128, 256], F32)
        on2 = A("on2", [128, 256], F32)
        gs = A("gs", [128, 512], F32)
        rr = A("rr", [128, 1024], F32)
        h1 = A("h1", [128, 1024], F32)
        h1b = A("h1b", [128, 1024], BF16)
        h1T = A("h1T", [128, 8, 128], BF16)
        pb = A("pb", [128, 256], BF16)
        pTt = A("pTt", [128, 2, 128], BF16)
        sg = A("sg", [128, 1024], F32)
        tt = A("tt", [128, 1024], F32)
        pT = A("pT", [128, 1024], BF16, True)
        pM = [A("pM%d" % i, [128, 512], F32, True) for i in range(2)]
        pGt = [A("pGt%d" % i, [128, 512], F32, True) for i in range(2)]
        pPe = [A("pPe%d" % i, [128, 512], F32, True) for i in range(2)]

        o, w = CST["ident"]
        P.dma("w", idf[:], g.cst[:, o:o + w], outs=[idf])
        P.dma("w", vec[:], g.vec[layer], outs=[vec])
        P.op("pool", lambda e: e.tensor_copy(out=idb[:], in_=idf[:]), outs=[idb], ins=[idf])
        wi = 0
        for (dst, src, nk) in ((wo, g.w_out, 8), (wg, g.w_pg, 8), (wp, g.w_pe, 2)):
            for k in range(nk):
                ws = wst[wi % 2]
                P.dma("w", ws[:], src[layer, k * 128:(k + 1) * 128, :], outs=[ws])
                eng = ("act", "dve")[wi % 2]
                wi += 1
                if eng == "act":
                    P.op("act", lambda e: e.copy(out=dst[:, k, :], in_=ws[:]), outs=[dst], ins=[ws])
                else:
                    P.op(eng, lambda e: e.tensor_copy(out=dst[:, k, :], in_=ws[:]), outs=[dst], ins=[ws])

        def rstd_from(col_in, col_out, n, scale, eps):
            P.op("dve", lambda e: e.tensor_scalar(out=st[:, col_out:col_out + n], in0=st[:, col_in:col_in + n], scalar1=scale, scalar2=eps,
                                                  op0=ALU.mult, op1=ALU.add), outs=[st], ins=[st])
            P.op("act", lambda e: e.activation(out=st[:, col_out:col_out + n], in_=st[:, col_out:col_out + n], func=AF.Ln), outs=[st], ins=[st])
            P.op("act", lambda e: e.activation(out=st[:, col_out:col_out + n], in_=st[:, col_out:col_out + n], func=AF.Exp, scale=-0.5), outs=[st], ins=[st])

        for ci in range(nch):
            r0 = ci * 128
            ts = ci % 2
            tsem = "ldt%d" % ts
            ysb, zb, gtb, hb_, pbf, ho = ys_s[ts], z_s[ts], gt_s[ts], h_s[ts], p_s[ts], ho_s[ts]
            P.dma(tsem, ysb[:], g.ys[r0:r0 + 128, :], outs=[ysb])
            P.dma(tsem, zb[:], g.z_tm[r0:r0 + 128, :], outs=[zb], first=False)
            P.dma(tsem, gtb[:], g.gate_tm[r0:r0 + 128, :], outs=[gtb], first=False)
            P.dma(tsem, hb_[:], h_src[r0:r0 + 128, :], outs=[hb_], first=False)
            P.dma(tsem, pbf[:], g.p[layer, r0:r0 + 128, :], outs=[pbf], first=False)
            P.op("act", lambda e: e.activation(out=gz[:], in_=zb[:], func=AF.Silu), outs=[gz], ins=[zb])
            P.op("pool", lambda e: e.tensor_tensor(out=yg[:], in0=ysb[:, 0:512], in1=gz[:], op=ALU.mult), outs=[yg], ins=[ysb, gz])
            P.op("pool", lambda e: e.tensor_tensor(out=sq[:, 0:512], in0=yg[:], in1=yg[:], op=ALU.mult), outs=[sq], ins=[yg])
            P.op("dve", lambda e: e.tensor_reduce(out=st[:, 0:1], in_=sq[:, 0:512], axis=AX.X, op=ALU.add), outs=[st], ins=[sq])
            rstd_from(0, 1, 1, 1.0 / 512.0, RMS_EPS)
            P.op("dve", lambda e: e.scalar_tensor_tensor(out=ycb[:, 0:512], in0=yg[:], scalar=st[:, 1:2], in1=VS("ssd_nw"), op0=ALU.mult, op1=ALU.mult),
                 outs=[ycb], ins=[yg, st, vec])
            P.op("act", lambda e: e.activation(out=gs[:], in_=gtb[:], func=AF.Silu), outs=[gs], ins=[gtb])
            O3 = ysb[:, 512:768].rearrange("p (h f) -> p h f", h=4)
            P.op("pool", lambda e: e.tensor_tensor(out=sq[:, 512:768], in0=ysb[:, 512:768], in1=ysb[:, 512:768], op=ALU.mult), outs=[sq], ins=[ysb])
            P.op("dve", lambda e: e.tensor_reduce(out=st[:, 4:8], in_=sq[:, 512:768].rearrange("p (h f) -> p h f", h=4), axis=AX.X, op=ALU.add),
                 outs=[st], ins=[sq])
            rstd_from(4, 8, 4, 1.0 / 64.0, RMS_EPS)
            P.op("dve", lambda e: e.tensor_tensor(out=on[:].rearrange("p (h f) -> p h f", h=4), in0=O3, in1=bc(st[:, 8:12], [128, 4, 64], 2), op=ALU.mult),
                 outs=[on], ins=[ysb, st])
            P.op("pool", lambda e: e.tensor_tensor(out=on2[:].rearrange("p (h f) -> p h f", h=4), in0=on[:].rearrange("p (h f) -> p h f", h=4),
                                                   in1=bc(VS("gla_nw"), [128, 4, 64], 1), op=ALU.mult), outs=[on2], ins=[on, vec])
            P.op("pool", lambda e: e.tensor_tensor(out=ycb[:, 512:768], in0=on2[:], in1=gs[:, 0:256], op=ALU.mult), outs=[ycb], ins=[on2, gs])
            R3 = ysb[:, 768:1024].rearrange("p (h f) -> p h f", h=4)
            P.op("dve", lambda e: e.tensor_reduce(out=st[:, 12:16], in_=R3, axis=AX.X, op=ALU.add), outs=[st], ins=[ysb])
            P.op("pool", lambda e: e.tensor_tensor(out=sq[:, 768:1024], in0=ysb[:, 768:1024], in1=ysb[:, 768:1024], op=ALU.mult), outs=[sq], ins=[ysb])
            P.op("dve", lambda e: e.tensor_reduce(out=st[:, 16:20], in_=sq[:, 768:1024].rearrange("p (h f) -> p h f", h=4), axis=AX.X, op=ALU.add),
                 outs=[st], ins=[sq])
            P.op("dve", lambda e: e.tensor_scalar(out=st[:, 12:16], in0=st[:, 12:16], scalar1=1.0 / 64.0, scalar2=None, op0=ALU.mult), outs=[st], ins=[st])
            P.op("dve", lambda e: e.tensor_tensor(out=st[:, 20:24], in0=st[:, 12:16], in1=st[:, 12:16], op=ALU.mult), outs=[st], ins=[st])
            P.op("dve", lambda e: e.scalar_tensor_tensor(out=st[:, 16:20], in0=st[:, 16:20], scalar=1.0 / 64.0, in1=st[:, 20:24], op0=ALU.mult, op1=ALU.subtract),
                 outs=[st], ins=[st])
            rstd_from(16, 24, 4, 1.0, LN_EPS)
            P.op("dve", lambda e: e.tensor_tensor(out=on[:].rearrange("p (h f) -> p h f", h=4), in0=R3, in1=bc(st[:, 12:16], [128, 4, 64], 2), op=ALU.subtract),
                 outs=[on], ins=[ysb, st])
            P.op("dve", lambda e: e.tensor_tensor(out=on2[:].rearrange("p (h f) -> p h f", h=4), in0=on[:].rearrange("p (h f) -> p h f", h=4),
                                                  in1=bc(st[:, 24:28], [128, 4, 64], 2), op=ALU.mult), outs=[on2], ins=[on, st])
            P.op("pool", lambda e: e.tensor_tensor(out=on[:], in0=on2[:], in1=VS("ret_nw"), op=ALU.mult), outs=[on], ins=[on2, vec])
            P.op("pool", lambda e: e.tensor_tensor(out=on2[:], in0=on[:], in1=VS("ret_nb"), op=ALU.add), outs=[on2], ins=[on, vec])
            P.op("pool", lambda e: e.tensor_tensor(out=ycb[:, 768:1024], in0=on2[:], in1=gs[:, 256:512], op=ALU.mult), outs=[ycb], ins=[on2, gs])
            for k in range(8):
                P.op("pe", lambda e: e.transpose(pT[:, k * 128:(k + 1) * 128], ycb[:, k * 128:(k + 1) * 128], idb[:]), outs=[pT], ins=[ycb, idb])
            P.op("act", lambda e: e.copy(out=ycT[:], in_=pT[:].rearrange("p (k t) -> p k t", k=8)), outs=[ycT], ins=[pT])
            for hf in range(2):
                for k in range(8):
                    P.op("pe", lambda e: e.matmul(pM[hf][:, :], lhsT=ycT[:, k, :], rhs=wo[:, k, hf * 512:(hf + 1) * 512], start=(k == 0), stop=(k == 7)),
                         outs=[pM[hf]], ins=[ycT, wo])
                P.op("dve", lambda e: e.scalar_tensor_tensor(out=rr[:, hf * 512:(hf + 1) * 512], in0=hb_[:, hf * 512:(hf + 1) * 512], scalar=DN_ALPHA,
                                                             in1=pM[hf][:, :], op0=ALU.mult, op1=ALU.add), outs=[rr], ins=[hb_, pM[hf]])
            P.op("dve", lambda e: e.tensor_reduce(out=st[:, 28:29], in_=rr[:], axis=AX.X, op=ALU.add), outs=[st], ins=[rr])
            P.op("pool", lambda e: e.tensor_tensor(out=sq[:], in0=rr[:], in1=rr[:], op=ALU.mult), outs=[sq], ins=[rr])
            P.op("dve", lambda e: e.tensor_reduce(out=st[:, 29:30], in_=sq[:], axis=AX.X, op=ALU.add), outs=[st], ins=[sq])
            P.op("dve", lambda e: e.tensor_scalar(out=st[:, 28:29], in0=st[:, 28:29], scalar1=1.0 / 1024.0, scalar2=None, op0=ALU.mult), outs=[st], ins=[st])
            P.op("dve", lambda e: e.tensor_tensor(out=st[:, 30:31], in0=st[:, 28:29], in1=st[:, 28:29], op=ALU.mult), outs=[st], ins=[st])
            P.op("dve", lambda e: e.scalar_tensor_tensor(out=st[:, 29:30], in0=st[:, 29:30], scalar=1.0 / 1024.0, in1=st[:, 30:31], op0=ALU.mult, op1=ALU.subtract),
                 outs=[st], ins=[st])
            rstd_from(29, 31, 1, 1.0, LN_EPS)
            P.op("dve", lambda e: e.tensor_scalar(out=rr[:], in0=rr[:], scalar1=st[:, 28:29], scalar2=st[:, 31:32], op0=ALU.subtract, op1=ALU.mult),
                 outs=[rr], ins=[rr, st])
            P.op("pool", lambda e: e.tensor_tensor(out=rr[:], in0=rr[:], in1=VS("ln_w"), op=ALU.mult), outs=[rr], ins=[rr, vec])
            P.op("pool", lambda e: e.tensor_tensor(out=h1[:], in0=rr[:], in1=VS("ln_b"), op=ALU.add), outs=[h1], ins=[rr, vec])
            P.op("act", lambda e: e.copy(out=h1b[:], in_=h1[:]), outs=[h1b], ins=[h1])
            for k in range(8):
                P.op("pe", lambda e: e.transpose(pT[:, k * 128:(k + 1) * 128], h1b[:, k * 128:(k + 1) * 128], idb[:]), outs=[pT], ins=[h1b, idb])
            P.op("act", lambda e: e.copy(out=h1T[:], in_=pT[:].rearrange("p (k t) -> p k t", k=8)), outs=[h1T], ins=[pT])
            for hf in range(2):
                for k in range(8):
                    P.op("pe", lambda e: e.matmul(pGt[hf][:, :], lhsT=h1T[:, k, :], rhs=wg[:, k, hf * 512:(hf + 1) * 512], start=(k == 0), stop=(k == 7)),
                         outs=[pGt[hf]], ins=[h1T, wg])
                P.op("dve", lambda e: e.tensor_tensor(out=sg[:, hf * 512:(hf + 1) * 512], in0=pGt[hf][:, :], in1=VS("b_pg")[:, hf * 512:(hf + 1) * 512], op=ALU.add),
                     outs=[sg], ins=[pGt[hf], vec])
            P.op("act", lambda e: e.activation(out=sg[:], in_=sg[:], func=AF.Sigmoid), outs=[sg], ins=[sg])
            P.op("act", lambda e: e.copy(out=pb[:], in_=pbf[:]), outs=[pb], ins=[pbf])
            for k in range(2):
                P.op("pe", lambda e: e.transpose(pT[:, k * 128:(k + 1) * 128], pb[:, k * 128:(k + 1) * 128], idb[:]), outs=[pT], ins=[pb, idb])
            P.op("act", lambda e: e.copy(out=pTt[:], in_=pT[:, 0:256].rearrange("p (k t) -> p k t", k=2)), outs=[pTt], ins=[pT])
            for hf in range(2):
                for k in range(2):
                    P.op("pe", lambda e: e.matmul(pPe[hf][:, :], lhsT=pTt[:, k, :], rhs=wp[:, k, hf * 512:(hf + 1) * 512], start=(k == 0), stop=(k == 1)),
                         outs=[pPe[hf]], ins=[pTt, wp])
                P.op("dve", lambda e: e.tensor_tensor(out=tt[:, hf * 512:(hf + 1) * 512], in0=pPe[hf][:, :], in1=sg[:, hf * 512:(hf + 1) * 512], op=ALU.mult),
                     outs=[tt], ins=[pPe[hf], sg])
            P.op("dve", lambda e: e.tensor_tensor(out=ho[:], in0=h1[:], in1=tt[:], op=ALU.add), outs=[ho], ins=[h1, tt])
            P.dma("st%d" % ts, h_dst[r0:r0 + 128, :], ho[:], ins=[ho])
    P.barrier()


def build_program(S=SEQ, n_layers=DEPTH, debug=False, phases=("p1", "c", "sb", "sf", "o")):
    nc = bass.Bass("TRN2", target_bir_lowering=False)
    g = Ctx()
    g.S = S
    inp = lambda n, s, d=F32: nc.dram_tensor(n, s, d, kind="ExternalInput").ap()
    g.x = inp("x", [S, D_MODEL])
    g.p = inp("p", [DEPTH, S, D_PLE])
    g.w_in = inp("w_in", [DEPTH, D_MODEL, NA])
    g.w_out = inp("w_out", [DEPTH, 1024, 1024])
    g.w_pg = inp("w_pg", [DEPTH, 1024, 1024])
    g.w_pe = inp("w_pe", [DEPTH, D_PLE, 1024])
    g.vec = inp("vec", [DEPTH, 128, NV])
    g.convp = inp("convp", [DEPTH, 128, 48])
    g.wa2 = inp("wa2", [DEPTH, 48, 128])
    g.cst = inp("cst", [128, NCST])
    g.rope = inp("rope", [S, 64])
    g.y = nc.dram_tensor("y", [S, D_MODEL], F32, kind="ExternalOutput").ap()
    kind = "ExternalOutput" if (debug or os.environ.get("SCR_EXT")) else "Internal"
    scr = lambda n, s, d=F32: nc.dram_tensor(n, s, d, kind=kind).ap()
    g.xbct = scr("xbct", [1024, S + 4])
    g.gqt = scr("gqt", [128, S])
    g.gkt = scr("gkt", [128, S])
    g.alrt = scr("alrt", [128, S])
    g.z_tm = scr("z_tm", [S, 512])
    g.gkvd_tm = scr("gkvd_tm", [S, 400])
    g.gate_tm = scr("gate_tm", [S, 512])
    g.rv_tm = scr("rv_tm", [S, 256])
    g.rk_tm = scr("rk_tm", [S, 256])
    g.rqkt = scr("rqkt", [512, S], BF16)
    g.bct = scr("bct", [512, S], BF16)
    g.x_tm = scr("x_tm", [S, 512])
    g.b_tm = scr("b_tm", [S, 256], BF16)
    g.yb = scr("yb", [S, 1024])
    g.ys = scr("ys", [S, 1024])
    g.h1 = scr("h1", [S, 1024])
    with contextlib.ExitStack() as es:
        P = Prog(nc, es)
        with contextlib.ExitStack() as es2:
            zt = _alloc(nc, es2, "zero_t", [128, 8, 2], F32)
            P.op("dve", lambda e: e.memset(zt[:], 0.0), outs=[zt])
            xv = g.xbct.rearrange("(n p) t -> p n t", p=128)
            P.dma("w", xv[:, :, 0:2], zt[:], ins=[zt])
            P.dma("w", xv[:, :, S + 2:S + 4], zt[:], ins=[zt])
        P.barrier()
        for layer in range(n_layers):
            if layer > 0:
                P.switch_dma_set(layer)
            h_src = g.x if layer == 0 else g.h1
            h_dst = g.y if layer == n_layers - 1 else g.h1
            if "p1" in phases:
                phase_p1(P, nc, g, layer, h_src)
                P.switch_pe_sem()
            if "c" in phases:
                phase_c(P, nc, g, layer)
                P.switch_pe_sem()
            if "sb" in phases:
                phase_sweep(P, nc, g, layer, "b")
                P.switch_pe_sem()
            if "sf" in phases:
                phase_sweep(P, nc, g, layer, "f")
                P.switch_pe_sem()
            if "o" in phases:
                phase_o(P, nc, g, layer, h_src, h_dst)
                P.switch_pe_sem()
        P.barrier()
        g.n_inst = P.n_inst
    return nc, g


def make_in_maps(inputs, S=SEQ, n_cores=8):
    f = lambda a: np.ascontiguousarray(np.asarray(a, dtype=np.float32))
    w_in = np.stack([_arrange_w_in(f(inputs["w_in"][i])) for i in range(DEPTH)])
    rep = lambda a: np.broadcast_to(f(a).reshape(1, -1), (128, f(a).size))
    vec = np.zeros((DEPTH, 128, NV), np.float32)
    convp = np.zeros((DEPTH, 128, 48), np.float32)
    wa2 = np.zeros((DEPTH, 48, 128), np.float32)
    for i in range(DEPTH):
        for name, src in [("dtb", inputs["dt_bias"][i]), ("alog", inputs["a_log"][i]), ("dsk", inputs["d_skip"][i]),
                          ("ssd_nw", inputs["ssd_norm_w"][i]), ("ba", inputs["gla_b_a"][i]), ("gla_nw", inputs["gla_norm_w"][i]),
                          ("ret_nw", inputs["ret_norm_w"][i]), ("ret_nb", inputs["ret_norm_b"][i]), ("ln_w", inputs["ln_w"][i]),
                          ("ln_b", inputs["ln_b"][i]), ("b_pg", inputs["b_pg"][i])]:
            o, w = VEC[name]
            vec[i, :, o:o + w] = rep(src)
        cwt = f(inputs["conv_w"][i]).T.reshape(8, 128, 5)
        convp[i, :, 0:40] = cwt.transpose(1, 0, 2).reshape(128, 40)
        convp[i, :, 40:48] = f(inputs["conv_b"][i]).reshape(8, 128).T
        wa2[i, 0:16] = f(inputs["gla_w_a2"][i, 0])
        wa2[i, 32:48] = f(inputs["gla_w_a2"][i, 1])
    cst, rope = _build_consts(S)
    x = f(inputs["x"])
    p = f(inputs["p"])
    shared = {"w_in": w_in, "w_out": f(inputs["w_out"]), "w_pg": f(inputs["w_pg"]), "w_pe": f(inputs["w_pe"]),
              "vec": vec, "convp": convp, "wa2": wa2, "cst": cst, "rope": rope}
    maps = []
    for c in range(n_cores):
        m = dict(shared)
        m["x"] = np.ascontiguousarray(x[c, :S])
        m["p"] = np.ascontiguousarray(p[:, c, :S])
        maps.append(m)
    return maps


def kernel(**inputs):
    nc, g = build_program()
    maps = make_in_maps(inputs)
    res = run_bass_kernel_spmd(nc, maps, core_ids=list(range(8)))
    return np.stack([r["y"] for r in res.results], axis=0).astype(np.float32)
```

```python
import contextlib
import os
import numpy as np
import concourse.bass as bass
import concourse.mybir as mybir
from concourse.bass_utils import run_bass_kernel_spmd

F32 = mybir.dt.float32
BF16 = mybir.dt.bfloat16
AF = mybir.ActivationFunctionType
ALU = mybir.AluOpType
AX = mybir.AxisListType

D_MODEL = 1024
SEQ = 8192
DEPTH = 2
D_PLE = 256
L = 128
SC = 512
DN_ALPHA = float((2 * DEPTH) ** 0.25)
LN_EPS = 1e-5
RMS_EPS = 1e-6
NEG = -30000.0

N_FM = 11
TM_GROUPS = [("z", 512), ("gkvd", 400), ("gate", 512), ("rqk", 512), ("rv", 256)]
TM_OFF = {}
_o = N_FM * 128
for _n, _w in TM_GROUPS:
    TM_OFF[_n] = _o
    _o += _w
NA = _o

VEC = {}
_o = 0
for _n, _w in [("dtb", 16), ("alog", 16), ("dsk", 8), ("ssd_nw", 512), ("ba", 256), ("gla_nw", 64),
               ("ret_nw", 256), ("ret_nb", 256), ("ln_w", 1024), ("ln_b", 1024), ("b_pg", 1024)]:
    VEC[_n] = (_o, _w)
    _o += _w
NV = _o

CST = {}
_o = 0
for _n, _w in [("ident", 128), ("ones", 128), ("hmask", 4), ("bd_gla", 256), ("bd_ret", 128), ("gL", 2)]:
    CST[_n] = (_o, _w)
    _o += _w
for _d in ("f", "b"):
    for _n, _w in [("M", 128), ("Mc", 128), ("nM", 128), ("mb", 1024), ("m01", 512), ("rdec", 512),
                   ("rgT", 256), ("rkd", 256)]:
        CST[_n + _d] = (_o, _w)
        _o += _w
NCST = _o


def _arrange_w_in(w):
    z = np.zeros((w.shape[0], 1), np.float32)
    zc = lambda n: np.zeros((w.shape[0], n), np.float32)
    cols = [w[:, 512:1536], w[:, 1552:1680], w[:, 1680:1808],
            w[:, 2320:2336], zc(16), w[:, 2336:2352], zc(80),
            w[:, 0:512],
            w[:, 1680:1808], w[:, 1808:2064], w[:, 1536:1552],
            w[:, 2064:2320], w[:, 3120:3376],
            w[:, 2352:2608], w[:, 2608:2864],
            w[:, 2864:3120]]
    out = np.concatenate(cols, axis=1)
    assert out.shape[1] == NA
    return np.ascontiguousarray(out)


def _build_consts(S):
    c = np.zeros((128, NCST), np.float32)

    def put(name, arr):
        o, w = CST[name]
        c[:, o:o + w] = np.asarray(arr, np.float32).reshape(128, w)

    r = np.arange(128)
    put("ident", np.eye(128))
    put("ones", np.ones((128, 128)))
    hm = np.zeros((128, 4)); hm[r, r // 32] = 1.0
    put("hmask", hm)
    put("bd_gla", (r[:, None] // 32 == (np.arange(256)[None, :] // 64)).astype(np.float32))
    put("bd_ret", (r[:, None] // 64 == (np.arange(128)[None, :] // 64)).astype(np.float32))
    gam = 1.0 - 2.0 ** (-5.0 - np.arange(4, dtype=np.float64))
    gl = np.zeros((128, 2))
    for t in range(2):
        gl[:, t] = gam[2 * t + r // 64] ** L
    put("gL", gl)
    s_ = r[:, None].astype(np.float64)
    t_ = r[None, :].astype(np.float64)
    for d in ("f", "b"):
        if d == "f":
            M = (s_ <= t_).astype(np.float64)
            allow = (s_ <= t_)
            dist = t_ - s_
            pq = t_ + 1.0
            ks = (L - 1.0) - r.astype(np.float64)
        else:
            M = (s_ >= t_).astype(np.float64)
            allow = (s_ > t_)
            dist = s_ - t_
            pq = L - t_
            ks = r.astype(np.float64)
        put("M" + d, M)
        put("Mc" + d, 1.0 - M)
        put("nM" + d, -M)
        mb = np.where(allow, 0.0, NEG)
        put("mb" + d, np.broadcast_to(mb[:, None, :], (128, 8, 128)))
        put("m01" + d, np.broadcast_to(allow.astype(np.float64)[:, None, :], (128, 4, 128)))
        rdec = np.zeros((128, 4, 128))
        for h in range(4):
            rdec[:, h, :] = np.where(allow, gam[h] ** np.maximum(dist, 0.0), 0.0) * 0.125
        put("rdec" + d, rdec)
        rg = np.zeros((128, 2, 128))
        for t in range(2):
            gh = gam[2 * t + r // 64]
            rg[:, t, :] = gh[:, None] ** np.broadcast_to(pq, (128, 128))
        put("rgT" + d, rg)
        rk = np.zeros((128, 256))
        for h in range(4):
            rk[:, h * 64:(h + 1) * 64] = (gam[h] ** ks)[:, None] * 0.125
        put("rkd" + d, rk)
    half = 32
    inv = (10000.0 ** (-np.arange(half, dtype=np.float32) / half)).astype(np.float32)
    ang = np.arange(S, dtype=np.float32)[:, None] * inv[None, :]
    rope = np.concatenate([np.cos(ang), np.sin(ang)], axis=1).astype(np.float32)
    return c, np.ascontiguousarray(rope)


class Res:
    __slots__ = ("name", "w", "rd", "const")

    def __init__(self, name, const=False):
        self.name = name
        self.w = None
        self.rd = {}
        self.const = const


class Tile:
    def __init__(self, h, name):
        self.h = h
        self.r = Res(name)

    def __getitem__(self, k):
        return self.h[k]


class Prog:
    ENG = ("pe", "act", "dve", "pool")

    NSET = 4

    def __init__(self, nc, es):
        self.nc = nc
        self.es = es
        self.e = {"pe": nc.tensor, "act": nc.scalar, "dve": nc.vector, "pool": nc.gpsimd, "sp": nc.sync}
        self.dnames = ("ldf0", "ldf1", "ldt0", "ldt1", "st0", "st1")
        self.eh = {n: [es.enter_context(nc.semaphore("s%d_%s" % (i, n))) for i in range(self.NSET)] for n in self.ENG}
        self.ec = {n: [0] * self.NSET for n in self.ENG}
        self.dh = {n: [es.enter_context(nc.semaphore("d%d_%s" % (i, n))) for i in range(self.NSET)] for n in self.dnames}
        self.dc = {n: [0] * self.NSET for n in self.dnames}
        self.si = 0
        self.sem = {n: self.eh[n][0] for n in self.ENG}
        self.cnt = {n: 0 for n in self.ENG}
        self.dsem = {n: self.dh[n][0] for n in self.dnames}
        self.dsem["w"] = es.enter_context(nc.semaphore("d_w"))
        self.dcnt = {n: 0 for n in self.dsem}
        self.epoch = {}
        for n in self.ENG:
            self.epoch["e" + n] = 0
        for n in self.dsem:
            self.epoch["d" + n] = 0
        self.waited = {n: {} for n in ("pe", "act", "dve", "pool", "sp")}
        self.n_inst = 0
        self.limit = int(os.environ.get('OP_LIMIT', '100000000'))
        self.log = os.environ.get('OP_LOG')

    def _bump(self, key):
        self.epoch[key] += 1
        for w in self.waited.values():
            w.pop(key, None)

    def switch_pe_sem(self):
        i = self.si
        for n in self.ENG:
            self.ec[n][i] = self.cnt[n]
        for n in self.dnames:
            self.dc[n][i] = self.dcnt[n]
        i = (i + 1) % self.NSET
        self.si = i
        for n in self.ENG:
            self.sem[n] = self.eh[n][i]
            self.cnt[n] = self.ec[n][i]
            self._bump("e" + n)
        for n in self.dnames:
            self.dsem[n] = self.dh[n][i]
            self.dcnt[n] = self.dc[n][i]
            self._bump("d" + n)

    def switch_dma_set(self, si):
        pass

    def _wait(self, eng, tok):
        kind, name, val, ep = tok
        if ep != self.epoch[kind + name]:
            return
        if kind == "e":
            if name == eng and eng == "pe":
                return
            sem = self.sem[name]
        else:
            sem = self.dsem[name]
            val = self.dcnt[name]
        key = kind + name
        if self.waited[eng].get(key, -1) >= val:
            return
        self.e[eng].wait_ge(sem, val)
        self.waited[eng][key] = val

    def _deps(self, eng, outs, ins):
        for t in ins:
            r = t.r
            if r.w is not None:
                self._wait(eng, r.w)
        for t in outs:
            r = t.r
            if r.w is not None:
                self._wait(eng, r.w)
            for tok in r.rd.values():
                self._wait(eng, tok)

    def _mark(self, tok, outs, ins):
        for t in ins:
            if not t.r.const:
                t.r.rd[tok[0] + tok[1]] = tok
        for t in outs:
            t.r.w = tok
            t.r.rd = {}

    def op(self, eng, fn, outs=(), ins=()):
        if self.n_inst >= self.limit:
            return
        if self.log:
            import sys
            print("OP", self.n_inst, eng, sys._getframe(1).f_lineno)
        self._deps(eng, outs, ins)
        inst = fn(self.e[eng])
        self.cnt[eng] += 1
        inst.then_inc(self.sem[eng], 1)
        self._mark(("e", eng, self.cnt[eng], self.epoch["e" + eng]), outs, ins)
        self.n_inst += 1

    def dma(self, semname, out, in_, outs=(), ins=(), q="sp", first=True):
        if self.n_inst >= self.limit:
            return
        if first and self.dcnt[semname] > 0:
            self._wait(q, ("d", semname, self.dcnt[semname], self.epoch["d" + semname]))
        if self.log:
            import sys
            print("DMA", self.n_inst, semname, sys._getframe(1).f_lineno)
        self._deps(q, outs, ins)
        inst = self.e[q].dma_start(out=out, in_=in_)
        self.dcnt[semname] += 16
        inst.then_inc(self.dsem[semname], 16)
        self._mark(("d", semname, self.dcnt[semname], self.epoch["d" + semname]), outs, ins)
        self.n_inst += 1

    def barrier(self):
        for eng in ("pe", "act", "dve", "pool", "sp"):
            for n in self.ENG:
                if n != eng and self.cnt[n] > 0:
                    self._wait(eng, ("e", n, self.cnt[n], self.epoch["e" + n]))
            for n in self.dsem:
                if self.dcnt[n] > 0:
                    self._wait(eng, ("d", n, self.dcnt[n], self.epoch["d" + n]))


class Ctx:
    pass


_UID = [0]


def _alloc(nc, es, name, shape, dt, psum=False):
    _UID[0] += 1
    name = "%s_%d" % (name, _UID[0])
    if psum:
        h = es.enter_context(nc.psum_tensor(name, shape, dt))
    else:
        h = es.enter_context(nc.sbuf_tensor(name, shape, dt))
    return Tile(h, name)


def bc(ap, shape, axis):
    return ap.unsqueeze(axis).to_broadcast(shape)


def phase_p1(P, nc, g, layer, h_src):
    S = g.S
    nsc = S // SC
    with contextlib.ExitStack() as es:
        A = lambda n, s, d, ps=False: _alloc(nc, es, n, s, d, ps)
        wb = A("p1_wb", [128, 8, NA], BF16)
        wst = [A("p1_wst%d" % i, [128, NA], F32) for i in range(2)]
        idf = A("p1_idf", [128, 128], F32)
        idb = A("p1_idb", [128, 128], BF16)
        hin = [A("p1_hin%d" % i, [128, 1024], F32) for i in range(2)]
        cs = [A("p1_cs%d" % i, [128, 4, 64], F32) for i in range(2)]
        hb = A("p1_hb", [128, 1024], BF16)
        hT = A("p1_hT", [128, 8, SC], BF16)
        stf = [A("p1_stf%d" % i, [128, SC], F32) for i in range(2)]
        stt = [A("p1_stt%d" % i, [128, 512], F32) for i in range(2)]
        rot = [A("p1_rot%d" % i, [128, 512], F32) for i in range(2)]
        ra = A("p1_ra", [128, 512], F32)
        rb_ = A("p1_rb", [128, 512], F32)
        rotb = A("p1_rotb", [128, 512], BF16)
        rqk = [A("p1_rqk%d" % i, [128, 4, SC], BF16) for i in range(2)]
        pT = A("p1_pT", [128, 1024], BF16, True)
        pacc = [A("p1_pacc%d" % i, [128, 512], F32, True) for i in range(4)]
        pR = A("p1_pR", [128, 1024], BF16, True)
        idf.r.const = True
        idb.r.const = True
        wb.r.const = True

        o, w = CST["ident"]
        P.dma("w", idf[:], g.cst[:, o:o + w], outs=[idf])
        P.op("dve", lambda e: e.tensor_copy(out=idb[:], in_=idf[:]), outs=[idb], ins=[idf])
        for k in range(8):
            ws = wst[k % 2]
            P.dma("w", ws[:], g.w_in[layer, k * 128:(k + 1) * 128, :], outs=[ws])
            eng = ("act", "dve")[k % 2]
            if eng == "act":
                P.op("act", lambda e, k=k, ws=ws: e.copy(out=wb[:, k, :], in_=ws[:]), outs=[wb], ins=[ws])
            else:
                P.op(eng, lambda e, k=k, ws=ws: e.tensor_copy(out=wb[:, k, :], in_=ws[:]), outs=[wb], ins=[ws])

        ev = [0]

        def evac(out_ap, in_ap, outs, ins):
            ev[0] += 1
            if ev[0] % 2:
                P.op("act", lambda e: e.copy(out=out_ap, in_=in_ap), outs=outs, ins=ins)
            else:
                P.op("dve", lambda e: e.tensor_copy(out=out_ap, in_=in_ap), outs=outs, ins=ins)

        it = 0
        sti = 0
        for j in range(nsc):
            t0 = j * SC
            P.dma("ldf%d" % (j % 2), cs[j % 2][:], g.rope[t0:t0 + SC, :].rearrange("(c p) f -> p c f", p=128), outs=[cs[j % 2]])
            for c in range(4):
                sl = it % 2
                it += 1
                r0 = t0 + c * 128
                P.dma("ldt%d" % sl, hin[sl][:], h_src[r0:r0 + 128, :], outs=[hin[sl]])
                P.op("dve", lambda e, sl=sl: e.tensor_copy(out=hb[:], in_=hin[sl][:]), outs=[hb], ins=[hin[sl]])
                for k in range(8):
                    P.op("pe", lambda e, k=k: e.transpose(pT[:, k * 128:(k + 1) * 128], hb[:, k * 128:(k + 1) * 128], idb[:]),
                         outs=[pT], ins=[hb, idb])
                evac(hT[:, :, c * 128:(c + 1) * 128], pT[:].rearrange("p (k t) -> p k t", k=8), [hT], [pT])
            for n in range(N_FM):
                pa = pacc[n % 4]
                for k in range(8):
                    P.op("pe", lambda e, k=k, n=n, pa=pa: e.matmul(pa[:, :], lhsT=wb[:, k, n * 128:(n + 1) * 128], rhs=hT[:, k, :],
                                                                   start=(k == 0), stop=(k == 7)), outs=[pa], ins=[wb, hT])
                sf = stf[sti % 2]
                ssem = "st%d" % (sti % 2)
                sti += 1
                evac(sf[:], pa[:, :], [sf], [pa])
                if n < 8:
                    dst = g.xbct[n * 128:(n + 1) * 128, 2 + t0:2 + t0 + SC]
                elif n == 8:
                    dst = g.gqt[:, t0:t0 + SC]
                elif n == 9:
                    dst = g.gkt[:, t0:t0 + SC]
                else:
                    dst = g.alrt[:, t0:t0 + SC]
                P.dma(ssem, dst, sf[:], ins=[sf], q="pool")
            rq = rqk[j % 2]
            for c in range(4):
                r0 = t0 + c * 128
                for gi, (gn, gw) in enumerate(TM_GROUPS):
                    pa = pacc[gi % 4]
                    off = TM_OFF[gn]
                    for k in range(8):
                        P.op("pe", lambda e, k=k, pa=pa, off=off, gw=gw, c=c: e.matmul(
                            pa[:, 0:gw], lhsT=hT[:, k, c * 128:(c + 1) * 128], rhs=wb[:, k, off:off + gw],
                            start=(k == 0), stop=(k == 7)), outs=[pa], ins=[wb, hT])
                    if gn != "rqk":
                        sf = stt[sti % 2]
                        ssem = "st%d" % (sti % 2)
                        sti += 1
                        evac(sf[:, 0:gw], pa[:, 0:gw], [sf], [pa])
                        dst = {"z": g.z_tm, "gkvd": g.gkvd_tm, "gate": g.gate_tm, "rv": g.rv_tm}[gn]
                        P.dma(ssem, dst[r0:r0 + 128, :], sf[:, 0:gw], ins=[sf], q="pool")
                    else:
                        ro = rot[sti % 2]
                        ssem = "st%d" % (sti % 2)
                        sti += 1
                        csl = cs[j % 2]
                        R4 = pa[:, :].rearrange("p (h a f) -> p h a f", h=8, a=2)
                        cosb = csl[:, c, 0:32].unsqueeze(1).unsqueeze(1).to_broadcast([128, 8, 2, 32])
                        sinb = csl[:, c, 32:64].unsqueeze(1).unsqueeze(1).to_broadcast([128, 8, 2, 32])
                        A4 = ra[:].rearrange("p (h a f) -> p h a f", h=8, a=2)
                        B4 = rb_[:].rearrange("p (h a f) -> p h a f", h=8, a=2)
                        O4 = ro[:].rearrange("p (h a f) -> p h a f", h=8, a=2)
                        P.op("dve", lambda e: e.tensor_tensor(out=A4, in0=R4, in1=cosb, op=ALU.mult), outs=[ra], ins=[pa, csl])
                        P.op("dve", lambda e: e.tensor_tensor(out=B4, in0=R4, in1=sinb, op=ALU.mult), outs=[rb_], ins=[pa, csl])
                        P.op("dve", lambda e: e.tensor_tensor(out=O4[:, :, 0, :], in0=A4[:, :, 0, :], in1=B4[:, :, 1, :], op=ALU.subtract),
                             outs=[ro], ins=[ra, rb_])
                        P.op("dve", lambda e: e.tensor_tensor(out=O4[:, :, 1, :], in0=A4[:, :, 1, :], in1=B4[:, :, 0, :], op=ALU.add),
                             outs=[ro], ins=[ra, rb_])
                        P.dma(ssem, g.rk_tm[r0:r0 + 128, :], ro[:, 256:512], ins=[ro], q="pool")
                        P.op("act", lambda e: e.copy(out=rotb[:], in_=ro[:]), outs=[rotb], ins=[ro])
                        for n in range(4):
                            P.op("pe", lambda e, n=n: e.transpose(pR[:, n * 128:(n + 1) * 128], rotb[:, n * 128:(n + 1) * 128], idb[:]),
                                 outs=[pR], ins=[rotb, idb])
                        evac(rq[:, :, c * 128:(c + 1) * 128], pR[:, 0:512].rearrange("p (n t) -> p n t", n=4), [rq], [pR])
            P.dma("st%d" % (j % 2), g.rqkt.rearrange("(n p) t -> p n t", p=128)[:, :, t0:t0 + SC], rq[:], ins=[rq], q="pool")
    P.barrier()


def phase_c(P, nc, g, layer):
    S = g.S
    nsc = S // SC
    with contextlib.ExitStack() as es:
        A = lambda n, s, d, ps=False: _alloc(nc, es, n, s, d, ps)
        idf = A("c_idf", [128, 128], F32)
        idb = A("c_idb", [128, 128], BF16)
        cw = A("c_cw", [128, 48], F32)
        xin = [A("c_xin%d" % i, [128, 8, SC + 4], F32) for i in range(2)]
        acc = [A("c_acc%d" % i, [128, SC], F32) for i in range(4)]
        xact = [A("c_xact%d" % i, [128, 4, SC], F32) for i in range(2)]
        bct = [A("c_bct%d" % i, [128, 4, SC], BF16) for i in range(2)]
        xtm = [A("c_xtm%d" % i, [128, 512], F32) for i in range(2)]
        btm = [A("c_btm%d" % i, [128, 256], BF16) for i in range(2)]
        pX = [A("c_pX%d" % i, [128, 512], F32, True) for i in range(2)]
        pB = [A("c_pB%d" % i, [128, 1024], BF16, True) for i in range(2)]
        idf.r.const = True
        idb.r.const = True
        cw.r.const = True
        o, w = CST["ident"]
        P.dma("w", idf[:], g.cst[:, o:o + w], outs=[idf])
        P.dma("w", cw[:], g.convp[layer], outs=[cw])
        P.op("dve", lambda e: e.tensor_copy(out=idb[:], in_=idf[:]), outs=[idb], ins=[idf])
        it = 0
        for j in range(nsc):
            t0 = j * SC
            xi = xin[j % 2]
            P.dma("ldf%d" % (j % 2), xi[:], g.xbct.rearrange("(n p) t -> p n t", p=128)[:, :, t0:t0 + SC + 4], outs=[xi])
            xa = xact[j % 2]
            bt = bct[j % 2]
            for n in range(8):
                ac = acc[n % 4]
                eng = "dve"
                P.op(eng, lambda e: e.tensor_scalar(out=ac[:], in0=xi[:, n, 0:SC], scalar1=cw[:, n * 5:n * 5 + 1],
                                                    scalar2=cw[:, 40 + n:41 + n], op0=ALU.mult, op1=ALU.add), outs=[ac], ins=[xi, cw])
                for k in range(1, 5):
                    P.op("dve", lambda e: e.scalar_tensor_tensor(out=ac[:], in0=xi[:, n, k:k + SC], scalar=cw[:, n * 5 + k:n * 5 + k + 1],
                                                                 in1=ac[:], op0=ALU.mult, op1=ALU.add), outs=[ac], ins=[xi, cw, ac])
                if n < 4:
                    P.op("act", lambda e: e.activation(out=xa[:, n, :], in_=ac[:], func=AF.Silu), outs=[xa], ins=[ac])
                else:
                    P.op("act", lambda e: e.activation(out=bt[:, n - 4, :], in_=ac[:], func=AF.Silu), outs=[bt], ins=[ac])
            P.dma("st%d" % (j % 2), g.bct.rearrange("(n p) t -> p n t", p=128)[:, :, t0:t0 + SC], bt[:], ins=[bt], q="pool")
            for c in range(4):
                r0 = t0 + c * 128
                sl = it % 2
                it += 1
                px = pX[sl]
                pb = pB[sl]
                for n in range(4):
                    P.op("pe", lambda e: e.transpose(px[:, n * 128:(n + 1) * 128], xa[:, n, c * 128:(c + 1) * 128], idf[:]),
                         outs=[px], ins=[xa, idf])
                for n in range(2):
                    P.op("pe", lambda e: e.transpose(pb[:, n * 128:(n + 1) * 128], bt[:, n, c * 128:(c + 1) * 128], idb[:]),
                         outs=[pb], ins=[bt, idb])
                xt = xtm[sl]
                bm = btm[sl]
                P.op("act", lambda e: e.copy(out=xt[:], in_=px[:, :]), outs=[xt], ins=[px])
                P.op("dve", lambda e: e.tensor_copy(out=bm[:], in_=pb[:, 0:256]), outs=[bm], ins=[pb])
                P.dma("st%d" % sl, g.x_tm[r0:r0 + 128, :], xt[:], ins=[xt], q="pool")
                P.dma("st%d" % sl, g.b_tm[r0:r0 + 128, :], bm[:], ins=[bm], q="pool")
    P.barrier()


def phase_sweep(P, nc, g, layer, d):
    S = g.S
    nsc = S // SC
    fwd = (d == "f")
    di = 0 if fwd else 1
    with contextlib.ExitStack() as es:
        A = lambda n, s, dt, ps=False: _alloc(nc, es, "s_" + n, s, dt, ps)
        NG = CST["gL"][0] + CST["gL"][1]
        d0 = CST["M" + d][0]
        ND = CST["rkd" + d][0] + CST["rkd" + d][1] - d0
        cg = A("cg", [128, NG], F32)
        cd = A("cd", [128, ND], F32)
        vsm = A("vsm", [128, 808], F32)
        negA = A("negA", [128, 16], F32)
        wa2 = A("wa2", [48, 128], F32)
        idb = A("idb", [128, 128], BF16)
        for t in (cg, cd, vsm, negA, wa2, idb):
            t.r.const = True
        CG = lambda n: cg[:, CST[n][0]:CST[n][0] + CST[n][1]]
        CD = lambda n: cd[:, CST[n + d][0] - d0:CST[n + d][0] - d0 + CST[n + d][1]]
        VS = lambda n: vsm[:, VEC[n][0]:VEC[n][0] + VEC[n][1]]
        bct_s = [A("bct%d" % i, [128, 4, SC], BF16) for i in range(2)]
        gqk_s = [A("gqk%d" % i, [128, 2, SC], F32) for i in range(2)]
        alr_s = [A("alr%d" % i, [48, SC], F32) for i in range(2)]
        rqk_s = [A("rqk%d" % i, [128, 4, SC], BF16) for i in range(2)]
        btm_s = [A("btm%d" % i, [128, 256], BF16) for i in range(2)]
        xtm_s = [A("xtm%d" % i, [128, 512], F32) for i in range(2)]
        gkvd_s = [A("gkvd%d" % i, [128, 400], F32) for i in range(2)]
        rk_s = [A("rk%d" % i, [128, 256], F32) for i in range(2)]
        rv_s = [A("rv%d" % i, [128, 256], F32) for i in range(2)]
        yb_s = [A("yb%d" % i, [128, 1024], F32) for i in range(2)] if fwd else None
        ycat = [A("ycat%d" % i, [128, 1024], F32) for i in range(2)]
        S_ssd = A("S_ssd", [128, 512], F32)
        Sb_ssd = A("Sb_ssd", [128, 512], BF16)
        S_gla = A("S_gla", [128, 256], F32)
        Sb_gla = A("Sb_gla", [128, 256], BF16)
        S_ret = A("S_ret", [128, 2, 128], F32)
        Sb_ret = A("Sb_ret", [128, 2, 128], BF16)
        dtx = A("dtx", [128, 8], F32)
        dte = A("dte", [128, 8], F32)
        dtt = A("dtt", [128, 8], F32)
        la = A("la", [128, 8], F32)
        E3 = A("E3", [128, 24], F32)
        c2 = A("c2", [128, 8], F32)
        rhs1 = A("rhs1", [128, 8, 128], F32)
        rhs2 = A("rhs2", [128, 8, 128], F32)
        decT = A("decT", [128, 8, 128], F32)
        scT = A("scT", [128, 8, 128], BF16)
        Vt = A("Vt", [128, 512], BF16)
        Vs = A("Vs", [128, 512], BF16)
        tmpz = A("tmpz", [128, 512], F32)
        tmpd = A("tmpd", [128, 512], F32)
        xgb = A("xgb", [128, 128], F32)
        ge = A("ge", [128, 128], F32)
        gl = A("gl", [128, 128], F32)
        E1 = A("E1", [128, 128], F32)
        E2 = A("E2", [128, 128], F32)
        E2t = A("E2t", [128, 128], F32)
        qd = A("qd", [128, 128], BF16)
        kd = A("kd", [128, 128], BF16)
        kdtm = A("kdtm", [128, 128], BF16)
        qbd = A("qbd", [128, 4, 128], BF16)
        gscm = A("gscm", [128, 4, 128], BF16)
        GVb = A("GVb", [128, 256], BF16)
        gt1 = A("gt1", [128, 256], F32)
        rscm = A("rscm", [128, 4, 128], BF16)
        qdT = A("qdT", [128, 2, 128], BF16)
        rqbd = A("rqbd", [128, 2, 2, 128], BF16)
        rkdt = A("rkdt", [128, 256], BF16)
        RVb = A("RVb", [128, 256], BF16)
        rt1 = A("rt1", [128, 2, 128], F32)
        bank = [_alloc(nc, es, "s_bank%d" % i, [128, 512], F32, True).h for i in range(8)]
        pD = [Tile(bank[0], "pD0"), Tile(bank[1], "pD1")]
        pPP = Tile(bank[2], "pPP")
        pG = Tile(bank[2], "pG")
        pXG = Tile(bank[2], "pXG")
        pYi = Tile(bank[3], "pYi")
        pZS = Tile(bank[4], "pZS")
        pSC = Tile(bank[5], "pSC")
        pPT = Tile(bank[6], "pPT")
        pPtm = Tile(bank[6], "pPtm")
        pYg = Tile(bank[6], "pYg")
        pYr = Tile(bank[7], "pYr")
        pS2 = Tile(bank[7], "pS2")
        pG.r = pPP.r
        pXG.r = pPP.r
        pPtm.r = pPT.r
        pYg.r = pPT.r
        pS2.r = pYr.r

        P.dma("w", cg[:], g.cst[:, 0:NG], outs=[cg])
        P.dma("w", cd[:], g.cst[:, d0:d0 + ND], outs=[cd])
        P.dma("w", vsm[:], g.vec[layer, :, 0:808], outs=[vsm])
        P.dma("w", wa2[:], g.wa2[layer], outs=[wa2])
        P.op("dve", lambda e: e.tensor_copy(out=idb[:], in_=CG("ident")), outs=[idb], ins=[cg])
        P.op("act", lambda e: e.activation(out=negA[:], in_=VS("alog"), func=AF.Exp), outs=[negA], ins=[vsm])
        P.op("dve", lambda e: e.tensor_scalar(out=negA[:], in0=negA[:], scalar1=-1.0, scalar2=None, op0=ALU.mult), outs=[negA], ins=[negA])
        for st in (S_ssd, Sb_ssd, S_gla, Sb_gla, S_ret, Sb_ret):
            P.op("dve", lambda e: e.memset(st[:], 0.0), outs=[st])
        ident_f = CG("ident")
        ones_f = CG("ones")

        order_sc = list(range(nsc)) if fwd else list(range(nsc - 1, -1, -1))
        order_c = list(range(4)) if fwd else [3, 2, 1, 0]
        it = 0
        for ji, j in enumerate(order_sc):
            t0 = j * SC
            fs = ji % 2
            fsem = "ldf%d" % fs
            P.dma(fsem, bct_s[fs][:], g.bct.rearrange("(n p) t -> p n t", p=128)[:, :, t0:t0 + SC], outs=[bct_s[fs]])
            P.dma(fsem, gqk_s[fs][:, 0, :], g.gqt[:, t0:t0 + SC], outs=[gqk_s[fs]], first=False)
            P.dma(fsem, gqk_s[fs][:, 1, :], g.gkt[:, t0:t0 + SC], outs=[gqk_s[fs]], first=False)
            P.dma(fsem, alr_s[fs][:], g.alrt[0:48, t0:t0 + SC], outs=[alr_s[fs]], first=False)
            P.dma(fsem, rqk_s[fs][:], g.rqkt.rearrange("(n p) t -> p n t", p=128)[:, :, t0:t0 + SC], outs=[rqk_s[fs]], first=False)
            for c in order_c:
                r0 = t0 + c * 128
                ts = it % 2
                it += 1
                tsem = "ldt%d" % ts
                btm, xtm, gkvd, rkt, rvt, yc = btm_s[ts], xtm_s[ts], gkvd_s[ts], rk_s[ts], rv_s[ts], ycat[ts]
                P.dma(tsem, btm[:], g.b_tm[r0:r0 + 128, :], outs=[btm])
                P.dma(tsem, xtm[:], g.x_tm[r0:r0 + 128, :], outs=[xtm], first=False)
                P.dma(tsem, gkvd[:], g.gkvd_tm[r0:r0 + 128, :], outs=[gkvd], first=False)
                P.dma(tsem, rkt[:], g.rk_tm[r0:r0 + 128, :], outs=[rkt], first=False)
                P.dma(tsem, rvt[:], g.rv_tm[r0:r0 + 128, :], outs=[rvt], first=False)
                if fwd:
                    P.dma(tsem, yb_s[ts][:], g.yb[r0:r0 + 128, :], outs=[yb_s[ts]], first=False)
                csl = slice(c * 128, (c + 1) * 128)
                BC = bct_s[fs]
                PARTS = os.environ.get('SWEEP_PARTS', 'ssd,gla,ret').split(',')
                if 'ssd' not in PARTS:
                    P.op('dve', lambda e: e.memset(yc[:, 0:512], 0.0), outs=[yc])
                if 'ssd' in PARTS:
                    P.op("dve", lambda e: e.tensor_tensor(out=dtx[:], in0=gkvd[:, 384 + di * 8:392 + di * 8],
                                                          in1=VS("dtb")[:, di * 8:di * 8 + 8], op=ALU.add), outs=[dtx], ins=[gkvd, vsm])
                    P.op("act", lambda e: e.activation(out=dte[:], in_=dtx[:], func=AF.Exp), outs=[dte], ins=[dtx])
                    P.op("act", lambda e: e.activation(out=dtt[:], in_=dte[:], func=AF.Ln, bias=1.0), outs=[dtt], ins=[dte])
                    P.op("dve", lambda e: e.tensor_tensor(out=la[:], in0=dtt[:], in1=negA[:, di * 8:di * 8 + 8], op=ALU.mult),
                         outs=[la], ins=[dtt, negA])
                    P.op("pe", lambda e: e.matmul(pPP[:, 0:8], lhsT=CD("M"), rhs=la[:], start=True, stop=True), outs=[pPP], ins=[cd, la])
                    P.op("pe", lambda e: e.matmul(pPP[:, 8:16], lhsT=CD("Mc"), rhs=la[:], start=True, stop=True), outs=[pPP], ins=[cd, la])
                    P.op("pe", lambda e: e.matmul(pPP[:, 16:24], lhsT=ones_f, rhs=la[:], start=True, stop=True), outs=[pPP], ins=[cg, la])
                    P.op("act", lambda e: e.activation(out=E3[:], in_=pPP[:, 0:24], func=AF.Exp), outs=[E3], ins=[pPP])
                    P.op("dve", lambda e: e.tensor_tensor(out=rhs1[:], in0=bc(la[:], [128, 8, 128], 2),
                                                           in1=bc(CD("M"), [128, 8, 128], 1), op=ALU.mult), outs=[rhs1], ins=[la, cd])
                    P.op("dve", lambda e: e.tensor_copy(out=rhs2[:], in_=bc(la[:], [128, 8, 128], 2)), outs=[rhs2], ins=[la])
                    mbv = CD("mb").rearrange("p (h t) -> p h t", h=8)
                    for hf in range(2):
                        hs = slice(hf * 4, hf * 4 + 4)
                        P.op("pe", lambda e: e.matmul(pD[hf][:, :], lhsT=ones_f, rhs=rhs1[:, hs, :], start=True, stop=False), outs=[pD[hf]], ins=[cg, rhs1])
                        P.op("pe", lambda e: e.matmul(pD[hf][:, :], lhsT=CD("nM"), rhs=rhs2[:, hs, :], start=False, stop=False), outs=[pD[hf]], ins=[cd, rhs2])
                        P.op("pe", lambda e: e.matmul(pD[hf][:, :], lhsT=ident_f, rhs=mbv[:, hs, :], start=False, stop=True), outs=[pD[hf]], ins=[cg, cd])
                        P.op("act", lambda e: e.activation(out=decT[:, hs, :], in_=pD[hf][:, :].rearrange("p (h t) -> p h t", h=4), func=AF.Exp),
                             outs=[decT], ins=[pD[hf]])
                    for gi in range(2):
                        P.op("pe", lambda e: e.matmul(pG[:, 128 + gi * 128:256 + gi * 128], lhsT=BC[:, gi, csl], rhs=BC[:, 2 + gi, csl],
                                                      start=True, stop=True), outs=[pG], ins=[BC])
                    Gv = pG[:, 128:384].rearrange("p (g t) -> p g t", g=2)
                    for gi in range(2):
                        P.op("dve", lambda e: e.tensor_tensor(out=scT[:, gi * 4:gi * 4 + 4, :], in0=bc(Gv[:, gi, :], [128, 4, 128], 1),
                                                              in1=decT[:, gi * 4:gi * 4 + 4, :], op=ALU.mult), outs=[scT], ins=[pG, decT])
                    P.op("dve", lambda e: e.tensor_tensor(out=c2[:], in0=dtt[:], in1=E3[:, 8:16], op=ALU.mult), outs=[c2], ins=[dtt, E3])
                    X3 = xtm[:].rearrange("p (h f) -> p h f", h=8)
                    P.op("dve", lambda e: e.tensor_tensor(out=Vt[:].rearrange("p (h f) -> p h f", h=8), in0=X3, in1=bc(dtt[:], [128, 8, 64], 2), op=ALU.mult),
                         outs=[Vt], ins=[xtm, dtt])
                    P.op("dve", lambda e: e.tensor_tensor(out=Vs[:].rearrange("p (h f) -> p h f", h=8), in0=X3, in1=bc(c2[:], [128, 8, 64], 2), op=ALU.mult),
                         outs=[Vs], ins=[xtm, c2])
                    for h in range(8):
                        P.op("pe", lambda e: e.matmul(pYi[:, h * 64:(h + 1) * 64], lhsT=scT[:, h, :], rhs=Vt[:, h * 64:(h + 1) * 64], start=True, stop=True),
                             outs=[pYi], ins=[scT, Vt])
                    for gi in range(2):
                        P.op("pe", lambda e: e.matmul(pZS[:, gi * 256:(gi + 1) * 256], lhsT=BC[:, 2 + gi, csl], rhs=Sb_ssd[:, gi * 256:(gi + 1) * 256],
                                                      start=True, stop=True), outs=[pZS], ins=[BC, Sb_ssd])
                    P.op("dve", lambda e: e.tensor_tensor(out=tmpz[:].rearrange("p (h f) -> p h f", h=8), in0=pZS[:, :].rearrange("p (h f) -> p h f", h=8),
                                                          in1=bc(E3[:, 0:8], [128, 8, 64], 2), op=ALU.mult), outs=[tmpz], ins=[pZS, E3])
                    P.op("dve", lambda e: e.tensor_tensor(out=yc[:, 0:512], in0=pYi[:, :], in1=tmpz[:], op=ALU.add), outs=[yc], ins=[pYi, tmpz])
                    if fwd:
                        P.op("dve", lambda e: e.tensor_tensor(out=tmpd[:].rearrange("p (h f) -> p h f", h=8), in0=X3, in1=bc(VS("dsk"), [128, 8, 64], 2), op=ALU.mult),
                             outs=[tmpd], ins=[xtm, vsm])
                        P.op("dve", lambda e: e.tensor_tensor(out=yc[:, 0:512], in0=yc[:, 0:512], in1=tmpd[:], op=ALU.add), outs=[yc], ins=[yc, tmpd])
                    for gi in range(2):
                        P.op("pe", lambda e: e.matmul(pZS[:, gi * 256:(gi + 1) * 256], lhsT=btm[:, gi * 128:(gi + 1) * 128], rhs=Vs[:, gi * 256:(gi + 1) * 256],
                                                      start=True, stop=True), outs=[pZS], ins=[btm, Vs])
                    P.op("dve", lambda e: e.tensor_tensor(out=S_ssd[:].rearrange("p (h f) -> p h f", h=8), in0=S_ssd[:].rearrange("p (h f) -> p h f", h=8),
                                                           in1=bc(E3[:, 16:24], [128, 8, 64], 2), op=ALU.mult), outs=[S_ssd], ins=[S_ssd, E3])
                    P.op("dve", lambda e: e.tensor_tensor(out=S_ssd[:], in0=S_ssd[:], in1=pZS[:, :], op=ALU.add), outs=[S_ssd], ins=[S_ssd, pZS])
                    P.op("act", lambda e: e.copy(out=Sb_ssd[:], in_=S_ssd[:]), outs=[Sb_ssd], ins=[S_ssd])
                if 'gla' not in PARTS:
                    P.op('dve', lambda e: e.memset(yc[:, 512:768], 0.0), outs=[yc])
                if 'gla' in PARTS:
                    AL = alr_s[fs]
                    GQK = gqk_s[fs]
                    P.op("pe", lambda e: e.matmul(pXG[:, 384:512], lhsT=AL[di * 32:di * 32 + 16, csl], rhs=wa2[di * 32:di * 32 + 16, :], start=True, stop=True),
                         outs=[pXG], ins=[AL, wa2])
                    P.op("dve", lambda e: e.tensor_tensor(out=xgb[:], in0=pXG[:, 384:512], in1=VS("ba")[:, di * 128:(di + 1) * 128], op=ALU.add),
                         outs=[xgb], ins=[pXG, vsm])
                    P.op("act", lambda e: e.activation(out=ge[:], in_=xgb[:], func=AF.Exp, scale=-1.0), outs=[ge], ins=[xgb])
                    P.op("act", lambda e: e.activation(out=gl[:], in_=ge[:], func=AF.Ln, bias=1.0), outs=[gl], ins=[ge])
                    P.op("pe", lambda e: e.matmul(pPT[:, 0:128], lhsT=gl[:], rhs=CD("M"), start=True, stop=True), outs=[pPT], ins=[gl, cd])
                    P.op("pe", lambda e: e.matmul(pPtm[:, 128:256], lhsT=CD("M"), rhs=gl[:], start=True, stop=True), outs=[pPtm], ins=[gl, cd])
                    P.op("act", lambda e: e.activation(out=E1[:], in_=pPT[:, 0:128], func=AF.Exp, scale=-1.0 / 16.0), outs=[E1], ins=[pPT])
                    P.op("act", lambda e: e.activation(out=E2[:], in_=pPT[:, 0:128], func=AF.Exp, scale=1.0 / 16.0), outs=[E2], ins=[pPT])
                    P.op("act", lambda e: e.activation(out=E2t[:], in_=pPtm[:, 128:256], func=AF.Exp, scale=1.0 / 16.0), outs=[E2t], ins=[pPtm])
                    P.op("dve", lambda e: e.scalar_tensor_tensor(out=qd[:], in0=GQK[:, 0, csl], scalar=float(32 ** -0.5), in1=E1[:], op0=ALU.mult, op1=ALU.mult),
                         outs=[qd], ins=[GQK, E1])
                    P.op("dve", lambda e: e.tensor_tensor(out=kd[:], in0=GQK[:, 1, csl], in1=E2[:], op=ALU.mult), outs=[kd], ins=[GQK, E2])
                    P.op("dve", lambda e: e.tensor_tensor(out=kdtm[:], in0=gkvd[:, 0:128], in1=E2t[:], op=ALU.mult), outs=[kdtm], ins=[gkvd, E2t])
                    P.op("dve", lambda e: e.tensor_tensor(out=qbd[:], in0=bc(qd[:], [128, 4, 128], 1), in1=bc(CG("hmask"), [128, 4, 128], 2), op=ALU.mult),
                         outs=[qbd], ins=[qd, cg])
                    P.op("act", lambda e: e.copy(out=GVb[:], in_=gkvd[:, 128:384]), outs=[GVb], ins=[gkvd])
                    P.op("pe", lambda e: e.matmul(pSC[:, :], lhsT=kd[:], rhs=qbd[:], start=True, stop=True), outs=[pSC], ins=[kd, qbd])
                    P.op("dve", lambda e: e.tensor_tensor(out=gscm[:], in0=pSC[:, :].rearrange("p (h t) -> p h t", h=4),
                                                          in1=CD("m01").rearrange("p (h t) -> p h t", h=4), op=ALU.mult), outs=[gscm], ins=[pSC, cd])
                    for h in range(4):
                        P.op("pe", lambda e: e.matmul(pYg[:, 256 + h * 64:256 + (h + 1) * 64], lhsT=qd[:], rhs=Sb_gla[:, h * 64:(h + 1) * 64], start=True, stop=False),
                             outs=[pYg], ins=[qd, Sb_gla])
                        P.op("pe", lambda e: e.matmul(pYg[:, 256 + h * 64:256 + (h + 1) * 64], lhsT=gscm[:, h, :], rhs=GVb[:, h * 64:(h + 1) * 64], start=False, stop=True),
                             outs=[pYg], ins=[gscm, GVb])
                    P.op("act", lambda e: e.copy(out=yc[:, 512:768], in_=pYg[:, 256:512]), outs=[yc], ins=[pYg])
                    eT = E1[:, 127:128] if fwd else E1[:, 0:1]
                    P.op("pe", lambda e: e.matmul(pS2[:, 256:512], lhsT=kdtm[:], rhs=GVb[:], start=True, stop=True), outs=[pS2], ins=[kdtm, GVb])
                    P.op("dve", lambda e: e.scalar_tensor_tensor(out=gt1[:], in0=pS2[:, 256:512], scalar=eT, in1=CG("bd_gla"), op0=ALU.mult, op1=ALU.mult),
                         outs=[gt1], ins=[pS2, E1, cg])
                    P.op("dve", lambda e: e.scalar_tensor_tensor(out=S_gla[:], in0=S_gla[:], scalar=eT, in1=gt1[:], op0=ALU.mult, op1=ALU.add),
                         outs=[S_gla], ins=[S_gla, E1, gt1])
                    P.op("dve", lambda e: e.tensor_copy(out=Sb_gla[:], in_=S_gla[:]), outs=[Sb_gla], ins=[S_gla])
                if 'ret' not in PARTS:
                    P.op('dve', lambda e: e.memset(yc[:, 768:1024], 0.0), outs=[yc])
                if 'ret' in PARTS:
                    RQ = rqk_s[fs]
                    hm2 = CG("bd_ret").rearrange("p (hh v) -> p hh v", hh=2)[:, :, 0]
                    P.op("dve", lambda e: e.tensor_tensor(out=rqbd[:], in0=RQ[:, 0:2, csl].unsqueeze(2).to_broadcast([128, 2, 2, 128]),
                                                          in1=hm2.unsqueeze(1).unsqueeze(3).to_broadcast([128, 2, 2, 128]), op=ALU.mult),
                         outs=[rqbd], ins=[RQ, cg])
                    for tl in range(2):
                        P.op("pe", lambda e: e.matmul(pSC[:, tl * 256:(tl + 1) * 256], lhsT=RQ[:, 2 + tl, csl], rhs=rqbd[:, tl, :, :], start=True, stop=True),
                             outs=[pSC], ins=[RQ, rqbd])
                    P.op("dve", lambda e: e.tensor_tensor(out=rscm[:], in0=pSC[:, :].rearrange("p (h t) -> p h t", h=4),
                                                          in1=CD("rdec").rearrange("p (h t) -> p h t", h=4), op=ALU.mult), outs=[rscm], ins=[pSC, cd])
                    P.op("dve", lambda e: e.tensor_tensor(out=qdT[:], in0=RQ[:, 0:2, csl], in1=CD("rgT").rearrange("p (n t) -> p n t", n=2), op=ALU.mult),
                         outs=[qdT], ins=[RQ, cd])
                    P.op("dve", lambda e: e.tensor_tensor(out=rkdt[:], in0=rkt[:], in1=CD("rkd"), op=ALU.mult), outs=[rkdt], ins=[rkt, cd])
                    P.op("act", lambda e: e.copy(out=RVb[:], in_=rvt[:]), outs=[RVb], ins=[rvt])
                    for h in range(4):
                        tl = h // 2
                        hc = slice((h % 2) * 64, (h % 2) * 64 + 64)
                        P.op("pe", lambda e: e.matmul(pYr[:, h * 64:(h + 1) * 64], lhsT=qdT[:, tl, :], rhs=Sb_ret[:, tl, hc], start=True, stop=False),
                             outs=[pYr], ins=[qdT, Sb_ret])
                        P.op("pe", lambda e: e.matmul(pYr[:, h * 64:(h + 1) * 64], lhsT=rscm[:, h, :], rhs=RVb[:, h * 64:(h + 1) * 64], start=False, stop=True),
                             outs=[pYr], ins=[rscm, RVb])
                    P.op("act", lambda e: e.copy(out=yc[:, 768:1024], in_=pYr[:, 0:256]), outs=[yc], ins=[pYr])
                    for tl in range(2):
                        P.op("pe", lambda e: e.matmul(pS2[:, 256 + tl * 128:256 + (tl + 1) * 128], lhsT=rkdt[:, tl * 128:(tl + 1) * 128],
                                                      rhs=RVb[:, tl * 128:(tl + 1) * 128], start=True, stop=True), outs=[pS2], ins=[rkdt, RVb])
                    P.op("dve", lambda e: e.tensor_tensor(out=rt1[:], in0=pS2[:, 256:512].rearrange("p (n t) -> p n t", n=2),
                                                          in1=bc(CG("bd_ret"), [128, 2, 128], 1), op=ALU.mult), outs=[rt1], ins=[pS2, cg])
                    for tl in range(2):
                        P.op("dve", lambda e: e.scalar_tensor_tensor(out=S_ret[:, tl, :], in0=S_ret[:, tl, :], scalar=CG("gL")[:, tl:tl + 1], in1=rt1[:, tl, :],
                                                                     op0=ALU.mult, op1=ALU.add), outs=[S_ret], ins=[S_ret, cg, rt1])
                    P.op("dve", lambda e: e.tensor_copy(out=Sb_ret[:], in_=S_ret[:]), outs=[Sb_ret], ins=[S_ret])
                if fwd:
                    P.op("dve", lambda e: e.tensor_tensor(out=yc[:], in0=yc[:], in1=yb_s[ts][:], op=ALU.add), outs=[yc], ins=[yc, yb_s[ts]])
                    P.dma("st%d" % ts, g.ys[r0:r0 + 128, :], yc[:], ins=[yc], q="pool")
                else:
                    P.dma("st%d" % ts, g.yb[r0:r0 + 128, :], yc[:], ins=[yc], q="pool")
    P.barrier()


def phase_o(P, nc, g, layer, h_src, h_dst):
    S = g.S
    nch = S // 128
    with contextlib.ExitStack() as es:
        A = lambda n, s, dt, ps=False: _alloc(nc, es, "o_" + n, s, dt, ps)
        idf = A("idf", [128, 128], F32)
        idb = A("idb", [128, 128], BF16)
        vec = A("vec", [128, NV], F32)
        wst = [A("wst%d" % i, [128, 1024], F32) for i in range(2)]
        wo = A("wo", [128, 8, 1024], BF16)
        wg = A("wg", [128, 8, 1024], BF16)
        wp = A("wp", [128, 2, 1024], BF16)
        for t in (idf, idb, vec, wo, wg, wp):
            t.r.const = True
        VS = lambda n: vec[:, VEC[n][0]:VEC[n][0] + VEC[n][1]]
        ys_s = [A("ys%d" % i, [128, 1024], F32) for i in range(2)]
        z_s = [A("z%d" % i, [128, 512], F32) for i in range(2)]
        gt_s = [A("gt%d" % i, [128, 512], F32) for i in range(2)]
        h_s = [A("h%d" % i, [128, 1024], F32) for i in range(2)]
        p_s = [A("p%d" % i, [128, 256], F32) for i in range(2)]
        ho_s = [A("ho%d" % i, [128, 1024], F32) for i in range(2)]
        gz = A("gz", [128, 512], F32)
        yg = A("yg", [128, 512], F32)
        sq = A("sq", [128, 1024], F32)
        st = A("st", [128, 32], F32)
        ycb = A("ycb", [128, 1024], BF16)
        ycT = A("ycT", [128, 8, 128], BF16)
        on = A("on", [128, 256], F32)
        on2 = A("on2", [128, 256], F32)
        gs = A("gs", [128, 512], F32)
        rr = A("rr", [128, 1024], F32)
        h1 = A("h1", [128, 1024], F32)
        h1b = A("h1b", [128, 1024], BF16)
        h1T = A("h1T", [128, 8, 128], BF16)
        pb = A("pb", [128, 256], BF16)
        pTt = A("pTt", [128, 2, 128], BF16)
        sg = A("sg", [128, 1024], F32)
        tt = A("tt", [128, 1024], F32)
        pT = A("pT", [128, 1024], BF16, True)
        pM = [A("pM%d" % i, [128, 512], F32, True) for i in range(2)]
        pGt = [A("pGt%d" % i, [128, 512], F32, True) for i in range(2)]
        pPe = [A("pPe%d" % i, [128, 512], F32, True) for i in range(2)]

        o, w = CST["ident"]
        P.dma("w", idf[:], g.cst[:, o:o + w], outs=[idf])
        P.dma("w", vec[:], g.vec[layer], outs=[vec])
        P.op("dve", lambda e: e.tensor_copy(out=idb[:], in_=idf[:]), outs=[idb], ins=[idf])
        wi = 0
        for (dst, src, nk) in ((wo, g.w_out, 8), (wg, g.w_pg, 8), (wp, g.w_pe, 2)):
            for k in range(nk):
                ws = wst[wi % 2]
                P.dma("w", ws[:], src[layer, k * 128:(k + 1) * 128, :], outs=[ws])
                eng = ("act", "dve")[wi % 2]
                wi += 1
                if eng == "act":
                    P.op("act", lambda e: e.copy(out=dst[:, k, :], in_=ws[:]), outs=[dst], ins=[ws])
                else:
                    P.op(eng, lambda e: e.tensor_copy(out=dst[:, k, :], in_=ws[:]), outs=[dst], ins=[ws])

        def rstd_from(col_in, col_out, n, scale, eps):
            P.op("dve", lambda e: e.tensor_scalar(out=st[:, col_out:col_out + n], in0=st[:, col_in:col_in + n], scalar1=scale, scalar2=eps,
                                                  op0=ALU.mult, op1=ALU.add), outs=[st], ins=[st])
            P.op("act", lambda e: e.activation(out=st[:, col_out:col_out + n], in_=st[:, col_out:col_out + n], func=AF.Ln), outs=[st], ins=[st])
            P.op("act", lambda e: e.activation(out=st[:, col_out:col_out + n], in_=st[:, col_out:col_out + n], func=AF.Exp, scale=-0.5), outs=[st], ins=[st])

        for ci in range(nch):
            r0 = ci * 128
            ts = ci % 2
            tsem = "ldt%d" % ts
            ysb, zb, gtb, hb_, pbf, ho = ys_s[ts], z_s[ts], gt_s[ts], h_s[ts], p_s[ts], ho_s[ts]
            P.dma(tsem, ysb[:], g.ys[r0:r0 + 128, :], outs=[ysb])
            P.dma(tsem, zb[:], g.z_tm[r0:r0 + 128, :], outs=[zb], first=False)
            P.dma(tsem, gtb[:], g.gate_tm[r0:r0 + 128, :], outs=[gtb], first=False)
            P.dma(tsem, hb_[:], h_src[r0:r0 + 128, :], outs=[hb_], first=False)
            P.dma(tsem, pbf[:], g.p[layer, r0:r0 + 128, :], outs=[pbf], first=False)
            P.op("act", lambda e: e.activation(out=gz[:], in_=zb[:], func=AF.Silu), outs=[gz], ins=[zb])
            P.op("dve", lambda e: e.tensor_tensor(out=yg[:], in0=ysb[:, 0:512], in1=gz[:], op=ALU.mult), outs=[yg], ins=[ysb, gz])
            P.op("dve", lambda e: e.tensor_tensor(out=sq[:, 0:512], in0=yg[:], in1=yg[:], op=ALU.mult), outs=[sq], ins=[yg])
            P.op("dve", lambda e: e.tensor_reduce(out=st[:, 0:1], in_=sq[:, 0:512], axis=AX.X, op=ALU.add), outs=[st], ins=[sq])
            rstd_from(0, 1, 1, 1.0 / 512.0, RMS_EPS)
            P.op("dve", lambda e: e.scalar_tensor_tensor(out=ycb[:, 0:512], in0=yg[:], scalar=st[:, 1:2], in1=VS("ssd_nw"), op0=ALU.mult, op1=ALU.mult),
                 outs=[ycb], ins=[yg, st, vec])
            P.op("act", lambda e: e.activation(out=gs[:], in_=gtb[:], func=AF.Silu), outs=[gs], ins=[gtb])
            O3 = ysb[:, 512:768].rearrange("p (h f) -> p h f", h=4)
            P.op("dve", lambda e: e.tensor_tensor(out=sq[:, 512:768], in0=ysb[:, 512:768], in1=ysb[:, 512:768], op=ALU.mult), outs=[sq], ins=[ysb])
            P.op("dve", lambda e: e.tensor_reduce(out=st[:, 4:8], in_=sq[:, 512:768].rearrange("p (h f) -> p h f", h=4), axis=AX.X, op=ALU.add),
                 outs=[st], ins=[sq])
            rstd_from(4, 8, 4, 1.0 / 64.0, RMS_EPS)
            P.op("dve", lambda e: e.tensor_tensor(out=on[:].rearrange("p (h f) -> p h f", h=4), in0=O3, in1=bc(st[:, 8:12], [128, 4, 64], 2), op=ALU.mult),
                 outs=[on], ins=[ysb, st])
            P.op("dve", lambda e: e.tensor_tensor(out=on2[:].rearrange("p (h f) -> p h f", h=4), in0=on[:].rearrange("p (h f) -> p h f", h=4),
                                                   in1=bc(VS("gla_nw"), [128, 4, 64], 1), op=ALU.mult), outs=[on2], ins=[on, vec])
            P.op("dve", lambda e: e.tensor_tensor(out=ycb[:, 512:768], in0=on2[:], in1=gs[:, 0:256], op=ALU.mult), outs=[ycb], ins=[on2, gs])
            R3 = ysb[:, 768:1024].rearrange("p (h f) -> p h f", h=4)
            P.op("dve", lambda e: e.tensor_reduce(out=st[:, 12:16], in_=R3, axis=AX.X, op=ALU.add), outs=[st], ins=[ysb])
            P.op("dve", lambda e: e.tensor_tensor(out=sq[:, 768:1024], in0=ysb[:, 768:1024], in1=ysb[:, 768:1024], op=ALU.mult), outs=[sq], ins=[ysb])
            P.op("dve", lambda e: e.tensor_reduce(out=st[:, 16:20], in_=sq[:, 768:1024].rearrange("p (h f) -> p h f", h=4), axis=AX.X, op=ALU.add),
                 outs=[st], ins=[sq])
            P.op("dve", lambda e: e.tensor_scalar(out=st[:, 12:16], in0=st[:, 12:16], scalar1=1.0 / 64.0, scalar2=None, op0=ALU.mult), outs=[st], ins=[st])
            P.op("dve", lambda e: e.tensor_tensor(out=st[:, 20:24], in0=st[:, 12:16], in1=st[:, 12:16], op=ALU.mult), outs=[st], ins=[st])
            P.op("dve", lambda e: e.scalar_tensor_tensor(out=st[:, 16:20], in0=st[:, 16:20], scalar=1.0 / 64.0, in1=st[:, 20:24], op0=ALU.mult, op1=ALU.subtract),
                 outs=[st], ins=[st])
            rstd_from(16, 24, 4, 1.0, LN_EPS)
            P.op("dve", lambda e: e.tensor_tensor(out=on[:].rearrange("p (h f) -> p h f", h=4), in0=R3, in1=bc(st[:, 12:16], [128, 4, 64], 2), op=ALU.subtract),
                 outs=[on], ins=[ysb, st])
            P.op("dve", lambda e: e.tensor_tensor(out=on2[:].rearrange("p (h f) -> p h f", h=4), in0=on[:].rearrange("p (h f) -> p h f", h=4),
                                                  in1=bc(st[:, 24:28], [128, 4, 64], 2), op=ALU.mult), outs=[on2], ins=[on, st])
            P.op("dve", lambda e: e.tensor_tensor(out=on[:], in0=on2[:], in1=VS("ret_nw"), op=ALU.mult), outs=[on], ins=[on2, vec])
            P.op("dve", lambda e: e.tensor_tensor(out=on2[:], in0=on[:], in1=VS("ret_nb"), op=ALU.add), outs=[on2], ins=[on, vec])
            P.op("dve", lambda e: e.tensor_tensor(out=ycb[:, 768:1024], in0=on2[:], in1=gs[:, 256:512], op=ALU.mult), outs=[ycb], ins=[on2, gs])
            for k in range(8):
                P.op("pe", lambda e: e.transpose(pT[:, k * 128:(k + 1) * 128], ycb[:, k * 128:(k + 1) * 128], idb[:]), outs=[pT], ins=[ycb, idb])
            P.op("act", lambda e: e.copy(out=ycT[:], in_=pT[:].rearrange("p (k t) -> p k t", k=8)), outs=[ycT], ins=[pT])
            for hf in range(2):
                for k in range(8):
                    P.op("pe", lambda e: e.matmul(pM[hf][:, :], lhsT=ycT[:, k, :], rhs=wo[:, k, hf * 512:(hf + 1) * 512], start=(k == 0), stop=(k == 7)),
                         outs=[pM[hf]], ins=[ycT, wo])
                P.op("dve", lambda e: e.scalar_tensor_tensor(out=rr[:, hf * 512:(hf + 1) * 512], in0=hb_[:, hf * 512:(hf + 1) * 512], scalar=DN_ALPHA,
                                                             in1=pM[hf][:, :], op0=ALU.mult, op1=ALU.add), outs=[rr], ins=[hb_, pM[hf]])
            P.op("dve", lambda e: e.tensor_reduce(out=st[:, 28:29], in_=rr[:], axis=AX.X, op=ALU.add), outs=[st], ins=[rr])
            P.op("dve", lambda e: e.tensor_tensor(out=sq[:], in0=rr[:], in1=rr[:], op=ALU.mult), outs=[sq], ins=[rr])
            P.op("dve", lambda e: e.tensor_reduce(out=st[:, 29:30], in_=sq[:], axis=AX.X, op=ALU.add), outs=[st], ins=[sq])
            P.op("dve", lambda e: e.tensor_scalar(out=st[:, 28:29], in0=st[:, 28:29], scalar1=1.0 / 1024.0, scalar2=None, op0=ALU.mult), outs=[st], ins=[st])
            P.op("dve", lambda e: e.tensor_tensor(out=st[:, 30:31], in0=st[:, 28:29], in1=st[:, 28:29], op=ALU.mult), outs=[st], ins=[st])
            P.op("dve", lambda e: e.scalar_tensor_tensor(out=st[:, 29:30], in0=st[:, 29:30], scalar=1.0 / 1024.0, in1=st[:, 30:31], op0=ALU.mult, op1=ALU.subtract),
                 outs=[st], ins=[st])
            rstd_from(29, 31, 1, 1.0, LN_EPS)
            P.op("dve", lambda e: e.tensor_scalar(out=rr[:], in0=rr[:], scalar1=st[:, 28:29], scalar2=st[:, 31:32], op0=ALU.subtract, op1=ALU.mult),
                 outs=[rr], ins=[rr, st])
            P.op("dve", lambda e: e.tensor_tensor(out=rr[:], in0=rr[:], in1=VS("ln_w"), op=ALU.mult), outs=[rr], ins=[rr, vec])
            P.op("dve", lambda e: e.tensor_tensor(out=h1[:], in0=rr[:], in1=VS("ln_b"), op=ALU.add), outs=[h1], ins=[rr, vec])
            P.op("act", lambda e: e.copy(out=h1b[:], in_=h1[:]), outs=[h1b], ins=[h1])
            for k in range(8):
                P.op("pe", lambda e: e.transpose(pT[:, k * 128:(k + 1) * 128], h1b[:, k * 128:(k + 1) * 128], idb[:]), outs=[pT], ins=[h1b, idb])
            P.op("act", lambda e: e.copy(out=h1T[:], in_=pT[:].rearrange("p (k t) -> p k t", k=8)), outs=[h1T], ins=[pT])
            for hf in range(2):
                for k in range(8):
                    P.op("pe", lambda e: e.matmul(pGt[hf][:, :], lhsT=h1T[:, k, :], rhs=wg[:, k, hf * 512:(hf + 1) * 512], start=(k == 0), stop=(k == 7)),
                         outs=[pGt[hf]], ins=[h1T, wg])
                P.op("dve", lambda e: e.tensor_tensor(out=sg[:, hf * 512:(hf + 1) * 512], in0=pGt[hf][:, :], in1=VS("b_pg")[:, hf * 512:(hf + 1) * 512], op=ALU.add),
                     outs=[sg], ins=[pGt[hf], vec])
            P.op("act", lambda e: e.activation(out=sg[:], in_=sg[:], func=AF.Sigmoid), outs=[sg], ins=[sg])
            P.op("act", lambda e: e.copy(out=pb[:], in_=pbf[:]), outs=[pb], ins=[pbf])
            for k in range(2):
                P.op("pe", lambda e: e.transpose(pT[:, k * 128:(k + 1) * 128], pb[:, k * 128:(k + 1) * 128], idb[:]), outs=[pT], ins=[pb, idb])
            P.op("act", lambda e: e.copy(out=pTt[:], in_=pT[:, 0:256].rearrange("p (k t) -> p k t", k=2)), outs=[pTt], ins=[pT])
            for hf in range(2):
                for k in range(2):
                    P.op("pe", lambda e: e.matmul(pPe[hf][:, :], lhsT=pTt[:, k, :], rhs=wp[:, k, hf * 512:(hf + 1) * 512], start=(k == 0), stop=(k == 1)),
                         outs=[pPe[hf]], ins=[pTt, wp])
                P.op("dve", lambda e: e.tensor_tensor(out=tt[:, hf * 512:(hf + 1) * 512], in0=pPe[hf][:, :], in1=sg[:, hf * 512:(hf + 1) * 512], op=ALU.mult),
                     outs=[tt], ins=[pPe[hf], sg])
            P.op("dve", lambda e: e.tensor_tensor(out=ho[:], in0=h1[:], in1=tt[:], op=ALU.add), outs=[ho], ins=[h1, tt])
            P.dma("st%d" % ts, h_dst[r0:r0 + 128, :], ho[:], ins=[ho], q="pool")
    P.barrier()


def build_program(S=SEQ, n_layers=DEPTH, debug=False, phases=("p1", "c", "sb", "sf", "o")):
    nc = bass.Bass("TRN2", target_bir_lowering=False)
    g = Ctx()
    g.S = S
    inp = lambda n, s, d=F32: nc.dram_tensor(n, s, d, kind="ExternalInput").ap()
    g.x = inp("x", [S, D_MODEL])
    g.p = inp("p", [DEPTH, S, D_PLE])
    g.w_in = inp("w_in", [DEPTH, D_MODEL, NA])
    g.w_out = inp("w_out", [DEPTH, 1024, 1024])
    g.w_pg = inp("w_pg", [DEPTH, 1024, 1024])
    g.w_pe = inp("w_pe", [DEPTH, D_PLE, 1024])
    g.vec = inp("vec", [DEPTH, 128, NV])
    g.convp = inp("convp", [DEPTH, 128, 48])
    g.wa2 = inp("wa2", [DEPTH, 48, 128])
    g.cst = inp("cst", [128, NCST])
    g.rope = inp("rope", [S, 64])
    g.y = nc.dram_tensor("y", [S, D_MODEL], F32, kind="ExternalOutput").ap()
    kind = "ExternalOutput" if (debug or os.environ.get("SCR_EXT")) else "Internal"
    scr = lambda n, s, d=F32: nc.dram_tensor(n, s, d, kind=kind).ap()
    g.xbct = scr("xbct", [1024, S + 4])
    g.gqt = scr("gqt", [128, S])
    g.gkt = scr("gkt", [128, S])
    g.alrt = scr("alrt", [128, S])
    g.z_tm = scr("z_tm", [S, 512])
    g.gkvd_tm = scr("gkvd_tm", [S, 400])
    g.gate_tm = scr("gate_tm", [S, 512])
    g.rv_tm = scr("rv_tm", [S, 256])
    g.rk_tm = scr("rk_tm", [S, 256])
    g.rqkt = scr("rqkt", [512, S], BF16)
    g.bct = scr("bct", [512, S], BF16)
    g.x_tm = scr("x_tm", [S, 512])
    g.b_tm = scr("b_tm", [S, 256], BF16)
    g.yb = scr("yb", [S, 1024])
    g.ys = scr("ys", [S, 1024])
    g.h1 = scr("h1", [S, 1024])
    with contextlib.ExitStack() as es:
        P = Prog(nc, es)
        with contextlib.ExitStack() as es2:
            zt = _alloc(nc, es2, "zero_t", [128, 8, 2], F32)
            P.op("dve", lambda e: e.memset(zt[:], 0.0), outs=[zt])
            xv = g.xbct.rearrange("(n p) t -> p n t", p=128)
            P.dma("w", xv[:, :, 0:2], zt[:], ins=[zt])
            P.dma("w", xv[:, :, S + 2:S + 4], zt[:], ins=[zt])
        P.barrier()
        for layer in range(n_layers):
            if layer > 0:
                P.switch_dma_set(layer)
            h_src = g.x if layer == 0 else g.h1
            h_dst = g.y if layer == n_layers - 1 else g.h1
            if "p1" in phases:
                phase_p1(P, nc, g, layer, h_src)
                P.switch_pe_sem()
            if "c" in phases:
                phase_c(P, nc, g, layer)
                P.switch_pe_sem()
            if "sb" in phases:
                phase_sweep(P, nc, g, layer, "b")
                P.switch_pe_sem()
            if "sf" in phases:
                phase_sweep(P, nc, g, layer, "f")
                P.switch_pe_sem()
            if "o" in phases:
                phase_o(P, nc, g, layer, h_src, h_dst)
                P.switch_pe_sem()
        P.barrier()
        g.n_inst = P.n_inst
    return nc, g


def make_in_maps(inputs, S=SEQ, n_cores=8):
    f = lambda a: np.ascontiguousarray(np.asarray(a, dtype=np.float32))
    w_in = np.stack([_arrange_w_in(f(inputs["w_in"][i])) for i in range(DEPTH)])
    rep = lambda a: np.broadcast_to(f(a).reshape(1, -1), (128, f(a).size))
    vec = np.zeros((DEPTH, 128, NV), np.float32)
    convp = np.zeros((DEPTH, 128, 48), np.float32)
    wa2 = np.zeros((DEPTH, 48, 128), np.float32)
    for i in range(DEPTH):
        for name, src in [("dtb", inputs["dt_bias"][i]), ("alog", inputs["a_log"][i]), ("dsk", inputs["d_skip"][i]),
                          ("ssd_nw", inputs["ssd_norm_w"][i]), ("ba", inputs["gla_b_a"][i]), ("gla_nw", inputs["gla_norm_w"][i]),
                          ("ret_nw", inputs["ret_norm_w"][i]), ("ret_nb", inputs["ret_norm_b"][i]), ("ln_w", inputs["ln_w"][i]),
                          ("ln_b", inputs["ln_b"][i]), ("b_pg", inputs["b_pg"][i])]:
            o, w = VEC[name]
            vec[i, :, o:o + w] = rep(src)
        cwt = f(inputs["conv_w"][i]).T.reshape(8, 128, 5)
        convp[i, :, 0:40] = cwt.transpose(1, 0, 2).reshape(128, 40)
        convp[i, :, 40:48] = f(inputs["conv_b"][i]).reshape(8, 128).T
        wa2[i, 0:16] = f(inputs["gla_w_a2"][i, 0])
        wa2[i, 32:48] = f(inputs["gla_w_a2"][i, 1])
    cst, rope = _build_consts(S)
    x = f(inputs["x"])
    p = f(inputs["p"])
    shared = {"w_in": w_in, "w_out": f(inputs["w_out"]), "w_pg": f(inputs["w_pg"]), "w_pe": f(inputs["w_pe"]),
              "vec": vec, "convp": convp, "wa2": wa2, "cst": cst, "rope": rope}
    maps = []
    for c in range(n_cores):
        m = dict(shared)
        m["x"] = np.ascontiguousarray(x[c, :S])
        m["p"] = np.ascontiguousarray(p[:, c, :S])
        maps.append(m)
    return maps


def kernel(**inputs):
    nc, g = build_program()
    maps = make_in_maps(inputs)
    res = run_bass_kernel_spmd(nc, maps, core_ids=list(range(8)))
    return np.stack([r["y"] for r in res.results], axis=0).astype(np.float32)
```
